# Optimizing a Trainium2 kernel written in Bass

```python
import jax, jax.numpy as jnp
from jax import lax
import numpy as np

D_MODEL = 1024
BATCH = 4
SEQ = 4096
DEPTH = 2
DEC_BATCH = 4
DEC_SEQ = 8192
PAST_LEN = 128

N_MEM = 256
EPS = 1e-6
ROPE_THETA = 500000.0
Q_BLOCK = 128

A_HEADS = 8
A_HEAD_DIM = 64
A_WIDTH = A_HEADS * A_HEAD_DIM
A_ROPE_DIM = A_HEAD_DIM // 4
DILATED_BRANCHES = ((128, 1), (512, 4), (2048, 16))

B_HEADS = 8
B_NOPE_DIM = 64
B_ROPE_DIM = 32
B_QK_DIM = B_NOPE_DIM + B_ROPE_DIM
B_V_DIM = 64
B_WIDTH = B_HEADS * B_V_DIM
Q_LORA = 256
KV_LORA = 128

MIX_WIDTH = A_WIDTH + B_WIDTH
IN_SPLITS = (A_WIDTH, 2 * A_WIDTH, 3 * A_WIDTH, 3 * A_WIDTH + Q_LORA, 3 * A_WIDTH + Q_LORA + KV_LORA)
IN_COLS = 3 * A_WIDTH + Q_LORA + KV_LORA + B_ROPE_DIM

M_HEADS = 4
M_HEAD_DIM = 128
M_WIDTH = M_HEADS * M_HEAD_DIM

D_FF = 4 * D_MODEL

kernel_name = 'hybrid_dilated_mla_memory_encoder'


def rmsnorm(x, g):
    xf = x.astype(jnp.float32)
    y = xf * lax.rsqrt(jnp.mean(xf * xf, axis=-1, keepdims=True) + EPS)
    return (y * g.astype(jnp.float32)).astype(x.dtype)


def rope(x, pos):
    r = x.shape[-1]
    inv = ROPE_THETA ** (-jnp.arange(0, r, 2, dtype=jnp.float32) / r)
    ang = pos.astype(jnp.float32)[:, None] * inv[None, :]
    cos = jnp.cos(ang)[:, None, :]
    sin = jnp.sin(ang)[:, None, :]
    xf = x.astype(jnp.float32)
    x1, x2 = xf[..., : r // 2], xf[..., r // 2:]
    return jnp.concatenate([x1 * cos - x2 * sin, x2 * cos + x1 * sin], axis=-1).astype(x.dtype)


def partial_rope(x, pos, rdim):
    return jnp.concatenate([rope(x[..., :rdim], pos), x[..., rdim:]], axis=-1)


def dilated_mixture_attention(q, k, v):
    b, s, h, dh = q.shape
    nblk = s // Q_BLOCK
    q_blocks = q.reshape(b, nblk, Q_BLOCK, h, dh).transpose(1, 0, 2, 3, 4)
    starts = jnp.arange(nblk, dtype=jnp.int32) * Q_BLOCK
    local = jnp.arange(Q_BLOCK, dtype=jnp.int32)
    offsets = [jnp.asarray(np.arange(-(w // 2), w // 2 + 1, d), dtype=jnp.int32) for w, d in DILATED_BRANCHES]

    def one_block(args):
        q_blk, start = args
        qpos = start + local
        maxes, dens, nums = [], [], []
        for offs in offsets:
            idx = qpos[:, None] + offs[None, :]
            valid = (idx >= 0) & (idx < s)
            idx = jnp.clip(idx, 0, s - 1)
            k_g = k[:, idx]
            v_g = v[:, idx]
            sc = jnp.einsum('bqhd,bqkhd->bhqk', q_blk, k_g).astype(jnp.float32)
            sc = jnp.where(valid[None, None], sc, -jnp.inf)
            mx = jnp.max(sc, axis=-1)
            p = jnp.exp(sc - mx[..., None])
            maxes.append(mx)
            dens.append(jnp.sum(p, axis=-1))
            nums.append(jnp.einsum('bhqk,bqkhd->bhqd', p, v_g.astype(jnp.float32)))
        m_all = jnp.max(jnp.stack(maxes), axis=0)
        wts = [jnp.exp(mx - m_all) for mx in maxes]
        den = sum(wi * di for wi, di in zip(wts, dens))
        num = sum(wi[..., None] * ni for wi, ni in zip(wts, nums))
        out = num / den[..., None]
        return out.transpose(0, 2, 1, 3).astype(q.dtype)

    out = lax.map(one_block, (q_blocks, starts))
    return out.transpose(1, 0, 2, 3, 4).reshape(b, s, h, dh)


def blocked_dense_attention(q, k, v):
    b, s, h, dq = q.shape
    nblk = s // Q_BLOCK
    q_blocks = q.reshape(b, nblk, Q_BLOCK, h, dq).transpose(1, 0, 2, 3, 4)

    def one_block(q_blk):
        sc = jnp.einsum('bqhd,bkhd->bhqk', q_blk, k).astype(jnp.float32)
        p = jax.nn.softmax(sc, axis=-1)
        return jnp.einsum('bhqk,bkhd->bqhd', p.astype(v.dtype), v)

    out = lax.map(one_block, q_blocks)
    return out.transpose(1, 0, 2, 3, 4).reshape(b, s, h, v.shape[-1])


def encoder_trunk(x, mem, params):
    (norm_mix_g, w_in, a_q_norm_g, a_k_norm_g, b_cq_norm_g, b_ckv_norm_g, w_uq, w_ukv,
     b_q_norm_g, b_k_norm_g, a_out_norm_g, b_out_norm_g, w_out,
     norm_mem_g, mem_kv_norm_g, m_wq, m_wkv, m_q_norm_g, m_k_norm_g, m_wo,
     norm_ffn_g, w_ff1, w_ff2) = params
    bsz, s, _ = x.shape
    pos = jnp.arange(s, dtype=jnp.int32)
    for l in range(DEPTH):
        h = rmsnorm(x, norm_mix_g[l])
        proj = h @ w_in[l]
        qa, ka, va, cq, ckv, kr = jnp.split(proj, IN_SPLITS, axis=-1)

        qa = qa.reshape(bsz, s, A_HEADS, A_HEAD_DIM)
        ka = ka.reshape(bsz, s, A_HEADS, A_HEAD_DIM)
        va = va.reshape(bsz, s, A_HEADS, A_HEAD_DIM)
        qa = partial_rope(rmsnorm(qa, a_q_norm_g[l]), pos, A_ROPE_DIM) * (A_HEAD_DIM ** -0.5)
        ka = partial_rope(rmsnorm(ka, a_k_norm_g[l]), pos, A_ROPE_DIM)
        oa = dilated_mixture_attention(qa, ka, va).reshape(bsz, s, A_WIDTH)

        qb = (rmsnorm(cq, b_cq_norm_g[l]) @ w_uq[l]).reshape(bsz, s, B_HEADS, B_QK_DIM)
        kvb = (rmsnorm(ckv, b_ckv_norm_g[l]) @ w_ukv[l]).reshape(bsz, s, B_HEADS, B_NOPE_DIM + B_V_DIM)
        k_nope, vb = kvb[..., :B_NOPE_DIM], kvb[..., B_NOPE_DIM:]
        k_rope = jnp.broadcast_to(kr[:, :, None, :], (bsz, s, B_HEADS, B_ROPE_DIM))
        kb = jnp.concatenate([k_nope, k_rope], axis=-1)
        qb = rmsnorm(qb, b_q_norm_g[l])
        kb = rmsnorm(kb, b_k_norm_g[l])
        qb = jnp.concatenate([qb[..., :B_NOPE_DIM], rope(qb[..., B_NOPE_DIM:], pos)], axis=-1) * (B_QK_DIM ** -0.5)
        kb = jnp.concatenate([kb[..., :B_NOPE_DIM], rope(kb[..., B_NOPE_DIM:], pos)], axis=-1)
        ob = blocked_dense_attention(qb, kb, vb).reshape(bsz, s, B_WIDTH)

        mixed = jnp.concatenate([rmsnorm(oa, a_out_norm_g[l]), rmsnorm(ob, b_out_norm_g[l])], axis=-1)
        x = x + mixed @ w_out[l]

        h = rmsnorm(x, norm_mem_g[l])
        mm = rmsnorm(mem, mem_kv_norm_g[l])
        qm = (h @ m_wq[l]).reshape(bsz, s, M_HEADS, M_HEAD_DIM)
        kvm = (mm @ m_wkv[l]).reshape(bsz, N_MEM, 2, M_HEADS, M_HEAD_DIM)
        km, vm = kvm[:, :, 0], kvm[:, :, 1]
        qm = rmsnorm(qm, m_q_norm_g[l]) * (M_HEAD_DIM ** -0.5)
        km = rmsnorm(km, m_k_norm_g[l])
        sc = jnp.einsum('bqhd,bkhd->bhqk', qm, km).astype(jnp.float32)
        p = jax.nn.softmax(sc, axis=-1)
        om = jnp.einsum('bhqk,bkhd->bqhd', p.astype(vm.dtype), vm).reshape(bsz, s, M_WIDTH)
        x = x + om @ m_wo[l]

        h = rmsnorm(x, norm_ffn_g[l])
        x = x + jnp.square(jax.nn.relu(h @ w_ff1[l])) @ w_ff2[l]
    return x


def setup_inputs(seed: int = 0) -> dict:
    key = jax.random.key(seed)
    ks = jax.random.split(key, 32)

    def dense(k, shape, fan_in):
        return jax.random.normal(k, shape, jnp.float32) * (fan_in ** -0.5)

    def gain(k, shape):
        return 1.0 + 0.02 * jax.random.normal(k, shape, jnp.float32)

    L = DEPTH
    return {
        'x_prompt': jax.random.normal(ks[0], (BATCH, SEQ, D_MODEL), jnp.float32),
        'x_sample': jax.random.normal(ks[1], (DEC_BATCH, DEC_SEQ, D_MODEL), jnp.float32),
        'mem_prompt': jax.random.normal(ks[2], (BATCH, N_MEM, D_MODEL), jnp.float32),
        'mem_sample': jax.random.normal(ks[3], (DEC_BATCH, N_MEM, D_MODEL), jnp.float32),
        'norm_mix_g': gain(ks[4], (L, D_MODEL)),
        'w_in': dense(ks[5], (L, D_MODEL, IN_COLS), D_MODEL),
        'a_q_norm_g': gain(ks[6], (L, A_HEAD_DIM)),
        'a_k_norm_g': gain(ks[7], (L, A_HEAD_DIM)),
        'b_cq_norm_g': gain(ks[8], (L, Q_LORA)),
        'b_ckv_norm_g': gain(ks[9], (L, KV_LORA)),
        'w_uq': dense(ks[10], (L, Q_LORA, B_HEADS * B_QK_DIM), Q_LORA),
        'w_ukv': dense(ks[11], (L, KV_LORA, B_HEADS * (B_NOPE_DIM + B_V_DIM)), KV_LORA),
        'b_q_norm_g': gain(ks[12], (L, B_QK_DIM)),
        'b_k_norm_g': gain(ks[13], (L, B_QK_DIM)),
        'a_out_norm_g': gain(ks[14], (L, A_WIDTH)),
        'b_out_norm_g': gain(ks[15], (L, B_WIDTH)),
        'w_out': dense(ks[16], (L, MIX_WIDTH, D_MODEL), MIX_WIDTH),
        'norm_mem_g': gain(ks[17], (L, D_MODEL)),
        'mem_kv_norm_g': gain(ks[18], (L, D_MODEL)),
        'm_wq': dense(ks[19], (L, D_MODEL, M_WIDTH), D_MODEL),
        'm_wkv': dense(ks[20], (L, D_MODEL, 2 * M_WIDTH), D_MODEL),
        'm_q_norm_g': gain(ks[21], (L, M_HEAD_DIM)),
        'm_k_norm_g': gain(ks[22], (L, M_HEAD_DIM)),
        'm_wo': dense(ks[23], (L, M_WIDTH, D_MODEL), M_WIDTH),
        'norm_ffn_g': gain(ks[24], (L, D_MODEL)),
        'w_ff1': dense(ks[25], (L, D_MODEL, D_FF), D_MODEL),
        'w_ff2': dense(ks[26], (L, D_FF, D_MODEL), D_FF),
    }


def reference(x_prompt, x_sample, mem_prompt, mem_sample, norm_mix_g, w_in, a_q_norm_g, a_k_norm_g,
              b_cq_norm_g, b_ckv_norm_g, w_uq, w_ukv, b_q_norm_g, b_k_norm_g, a_out_norm_g, b_out_norm_g,
              w_out, norm_mem_g, mem_kv_norm_g, m_wq, m_wkv, m_q_norm_g, m_k_norm_g, m_wo,
              norm_ffn_g, w_ff1, w_ff2):
    params = (norm_mix_g, w_in, a_q_norm_g, a_k_norm_g, b_cq_norm_g, b_ckv_norm_g, w_uq, w_ukv,
              b_q_norm_g, b_k_norm_g, a_out_norm_g, b_out_norm_g, w_out,
              norm_mem_g, mem_kv_norm_g, m_wq, m_wkv, m_q_norm_g, m_k_norm_g, m_wo,
              norm_ffn_g, w_ff1, w_ff2)
    y_prompt = encoder_trunk(x_prompt, mem_prompt, params)
    y_sample = encoder_trunk(x_sample, mem_sample, params)
    return (y_prompt, y_sample)
```

```python
import numpy as np
import ml_dtypes
from contextlib import ExitStack
import concourse.bass as bass
import concourse.mybir as mybir
from concourse.bass_utils import run_bass_kernel_spmd

F32 = mybir.dt.float32
BF16 = mybir.dt.bfloat16
AF = mybir.ActivationFunctionType
ALU = mybir.AluOpType
AX = mybir.AxisListType

D = 1024
DEPTH = 2
NMEM = 256
EPS = 1e-6
THETA = 500000.0
IN_COLS = 1952
DFF = 4096
SMAX = 8192

WNAMES = [
    ("norm_mix_g", [DEPTH, D]), ("w_in", [DEPTH, D, IN_COLS]), ("a_q_norm_g", [DEPTH, 64]),
    ("a_k_norm_g", [DEPTH, 64]), ("b_cq_norm_g", [DEPTH, 256]), ("b_ckv_norm_g", [DEPTH, 128]),
    ("w_uq", [DEPTH, 256, 768]), ("w_ukv", [DEPTH, 128, 1024]), ("b_q_norm_g", [DEPTH, 96]),
    ("b_k_norm_g", [DEPTH, 96]), ("a_out_norm_g", [DEPTH, 512]), ("b_out_norm_g", [DEPTH, 512]),
    ("w_out", [DEPTH, D, D]), ("norm_mem_g", [DEPTH, D]), ("mem_kv_norm_g", [DEPTH, D]),
    ("m_wq", [DEPTH, D, 512]), ("m_wkv", [DEPTH, D, 1024]), ("m_q_norm_g", [DEPTH, 128]),
    ("m_k_norm_g", [DEPTH, 128]), ("m_wo", [DEPTH, 512, D]), ("norm_ffn_g", [DEPTH, D]),
    ("w_ff1", [DEPTH, D, DFF]), ("w_ff2", [DEPTH, DFF, D]),
]


class Sem:
    def __init__(self, h, is_dma):
        self.h = h
        self.n = 0
        self.is_dma = is_dma


class Buf:
    def __init__(self, name=""):
        self.name = name
        self.w = None
        self.r = {}


class KB:
    def __init__(self, nc, es):
        self.nc = nc
        self.es = es
        self.eng = {"pe": nc.tensor, "act": nc.scalar, "dve": nc.vector, "pool": nc.gpsimd, "sp": nc.sync}
        self.esem = {}
        for e in ["pe", "act", "dve", "pool"]:
            self.esem[e] = self.newsem("c_" + e, False)
        self.waited = {e: {} for e in self.eng}
        self.free_dsems = []
        self.assigned = []
        self.ndsem = 0
        self.store_marker = None

    def newsem(self, name, is_dma=True):
        h = self.es.enter_context(self.nc.semaphore(name))
        s = Sem(h, is_dma)
        if not hasattr(self, "allsems"):
            self.allsems = []
        self.allsems.append(s)
        return s

    def barrier(self):
        for e in self.eng:
            for s in self.allsems:
                if s.n > 0 and self.waited[e].get(s, 0) < s.n:
                    self.eng[e].wait_ge(s.h, s.n)
                    self.waited[e][s] = s.n
        for b in self.assigned:
            self.free_dsems.append(b.dsem)
            b.dsem = None
        self.assigned = []

    def _deps(self, e, reads, writes):
        deps = {}

        def add(s, v):
            if deps.get(s, 0) < v:
                deps[s] = v

        for b in reads:
            if b.w:
                add(*b.w)
        for b in writes:
            if b.w:
                add(*b.w)
            for s, v in b.r.items():
                add(s, v)
        for s, v in deps.items():
            if e == "pe" and s is self.esem["pe"]:
                continue
            if s.is_dma:
                v = s.n
            if self.waited[e].get(s, 0) >= v:
                continue
            self.eng[e].wait_ge(s.h, v)
            self.waited[e][s] = v

    def op(self, e, fn, reads=(), writes=()):
        self._deps(e, reads, writes)
        s = self.esem[e]
        s.n += 1
        fn(self.eng[e]).then_inc(s.h, 1)
        for b in reads:
            b.r[s] = s.n
        for b in writes:
            b.w = (s, s.n)
            b.r = {}

    def dsem(self, buf):
        if getattr(buf, "dsem", None) is None:
            if self.free_dsems:
                buf.dsem = self.free_dsems.pop()
            else:
                self.ndsem += 1
                buf.dsem = self.newsem("d%d" % self.ndsem)
            self.assigned.append(buf)
        return buf.dsem

    def dma(self, q, out, in_, reads, writes, sem, nowaw=False, **kw):
        own = reads[0] if sem is self.store_marker else writes[0]
        sem = self.dsem(own)
        if nowaw:
            for b in writes:
                b.w = None
        self._deps(q, reads, writes)
        sem.n += 16
        self.eng[q].dma_start(out=out, in_=in_, **kw).then_inc(sem.h, 16)
        for b in reads:
            b.r[sem] = sem.n
        for b in writes:
            b.w = (sem, sem.n)
            b.r = {}

    def final_wait(self, e, bufs):
        self._deps(e, bufs, ())


class Rot:
    def __init__(self, items):
        self.items = items
        self.i = 0

    def next(self):
        it = self.items[self.i % len(self.items)]
        self.i += 1
        return it


def host_consts(S):
    c = {}
    c["c_ident"] = np.eye(128, dtype=np.float32)
    c["c_ones"] = np.ones((128, 128), np.float32)
    blk = np.zeros((128, 128), np.float32)
    blk[:64, :64] = 1
    blk[64:, 64:] = 1
    c["c_blk2"] = blk
    ra = np.zeros((128, 128), np.float32)
    for base in (0, 64):
        for i in range(8):
            ra[base + i + 8, base + i] = -1.0
            ra[base + i, base + i + 8] = 1.0
    c["c_ra"] = ra
    rb = np.zeros((128, 128), np.float32)
    for i in range(16):
        rb[64 + i + 16, 64 + i] = -1.0
        rb[64 + i, 64 + i + 16] = 1.0
    c["c_rb"] = rb
    ee = np.zeros((128, 128), np.float32)
    for i in range(32):
        ee[i, 64 + i] = 1.0
    c["c_e"] = ee
    pos = np.arange(S, dtype=np.float32)

    def tables(r):
        inv = (np.float32(THETA) ** (-np.arange(0, r, 2, dtype=np.float32) / np.float32(r))).astype(np.float32)
        ang = (pos[None, :] * inv[:, None]).astype(np.float32).astype(np.float64)
        return np.cos(ang).astype(np.float32), np.sin(ang).astype(np.float32)

    ca, sa = tables(16)
    cA = np.ones((128, S), np.float32)
    sA = np.zeros((128, S), np.float32)
    for base in (0, 64):
        cA[base:base + 8] = ca
        cA[base + 8:base + 16] = ca
        sA[base:base + 8] = sa
        sA[base + 8:base + 16] = sa
    cb, sb_ = tables(32)
    cB = np.ones((128, S), np.float32)
    sB = np.zeros((128, S), np.float32)
    cB[64:80] = cb
    cB[80:96] = cb
    sB[64:80] = sb_
    sB[80:96] = sb_
    c["c_ropeA_c"], c["c_ropeA_s"], c["c_ropeB_c"], c["c_ropeB_s"] = cA, sA, cB, sB
    deltas = list(range(-1024, 1408 + 1, 128))
    kl = np.arange(128)[:, None]
    ql = np.arange(512)[None, :]
    m = np.zeros((len(deltas), 128, 512), np.float32)
    for i, dl in enumerate(deltas):
        off = dl + kl - ql
        a = np.abs(off)
        m[i] = (a <= 64).astype(np.float32) + ((a <= 256) & (off % 4 == 0)) + ((a <= 1024) & (off % 16 == 0))
    c["c_band"] = m
    return c


CONST_SHAPES = lambda S: {
    "c_ident": [128, 128], "c_ones": [128, 128], "c_blk2": [128, 128], "c_ra": [128, 128], "c_rb": [128, 128],
    "c_e": [128, 128], "c_ropeA_c": [128, S], "c_ropeA_s": [128, S], "c_ropeB_c": [128, S], "c_ropeB_s": [128, S],
    "c_band": [20, 128, 512],
}


def build(S, debug=False, nlayers=DEPTH, phases=None):
    NB = S // 128
    NT = S // 512
    nc = bass.Bass("TRN2", target_bir_lowering=False)

    def din(name, shape):
        return nc.dram_tensor(name, list(shape), F32, kind="ExternalInput").ap()

    x_in = din("x", [S, D])
    kmask_in = din("kmask", [128, NB])
    mem_in = din("mem", [NMEM, D])
    W = {n: din(n, s) for n, s in WNAMES}
    C = {n: din(n, s) for n, s in CONST_SHAPES(S).items()}
    y = nc.dram_tensor("y", [S, D], F32, kind="ExternalOutput").ap()
    skind = "ExternalOutput" if debug else "Internal"

    def dscr(name, shape, dt):
        return nc.dram_tensor(name, list(shape), dt, kind=skind).ap()

    QA = dscr("s_qa", [4, 128, S], BF16)
    KA = dscr("s_ka", [4, 128, S], BF16)
    VA = dscr("s_va", [S, 8 * 65], BF16)
    QB = dscr("s_qb", [8, 96, S], BF16)
    KBd = dscr("s_kb", [8, 96, S], BF16)
    VB = dscr("s_vb", [S, 8 * 65], BF16)
    OT = dscr("s_ot", [8, 128, S], F32)

    B_QA, B_KA, B_VA, B_QB, B_KB, B_VB, B_OT = [Buf(n) for n in ["QA", "KA", "VA", "QB", "KB", "VB", "OT"]]
    B_Y = [Buf("Y%d" % i) for i in range(S // 256)]
    B_XIN = [Buf("X%d" % i) for i in range(S // 256)]

    with ExitStack() as es_all:
        k = KB(nc, es_all)
        sem_ld = [object() for i in range(6)]
        sem_st = object()
        k.store_marker = sem_st

        def sb_p(name, shape, dt=F32):
            return es_all.enter_context(nc.sbuf_tensor(name, shape, dt))

        ident = sb_p("ident", [128, 128], BF16)
        ones_b = sb_p("ones_b", [128, 128], BF16)
        blk2 = sb_p("blk2", [128, 128], BF16)
        ra = sb_p("ra", [128, 128], BF16)
        rb = sb_p("rb", [128, 128], BF16)
        esel = sb_p("esel", [128, 128], BF16)
        ones_f = sb_p("ones_f", [128, 128], F32)
        cst = sb_p("cst", [128, 8], F32)
        kmask = sb_p("kmask_sb", [128, NB], F32)
        tinyrow = sb_p("tinyrow", [1, 512], F32)
        B_const = Buf("const")
        k.dma("sp", ones_f[:], C["c_ones"][:, :], [], [B_const], sem_ld[0])
        k.dma("sp", kmask[:], kmask_in[:, :], [], [B_const], sem_ld[0])
        for i, v in enumerate([EPS, 0.0, 1e-18, -0.5 * np.log(64.0), -0.5 * np.log(96.0), -0.5 * np.log(128.0)]):
            k.op("dve", lambda e, i=i, v=v: e.memset(cst[:, i:i + 1], float(v)), [], [B_const])
        C_EPS, C_ZERO, C_TINY, C_L64, C_L96, C_L128 = [cst[:, i:i + 1] for i in range(6)]
        k.op("dve", lambda e: e.memset(tinyrow[:], 1e-30), [], [B_const])

        uid = [0]

        def load_weights(items):
            uid[0] += 1
            with ExitStack() as es_s:
                stg = Rot([(es_s.enter_context(nc.sbuf_tensor("stg%d_%d" % (i, uid[0]), [128, 4096], F32)), Buf()) for i in range(2)])
                B_d = Buf()
                for i, it in enumerate(items):
                    dst, src_, P, n = it[0:4]
                    c = it[4] if len(it) > 4 else 1
                    s_, B_s = stg.next()
                    sv = s_[0:P, 0:n] if c == 1 else s_[0:P, 0:n].rearrange("p (c n) -> p c n", c=c)
                    k.dma("sp", sv, src_, [], [B_s], sem_ld[0])
                    k.op("dve" if i % 3 != 2 else "pool", lambda e, dst=dst, sv=sv: e.tensor_copy(out=dst, in_=sv), [B_s], [B_d])
                k.barrier()

        load_weights([(t_[:], C[n_][:, :], 128, 128) for t_, n_ in [(ident, "c_ident"), (ones_b, "c_ones"), (blk2, "c_blk2"), (ra, "c_ra"), (rb, "c_rb"), (esel, "c_e")]])

        def norm_transpose(es, tag):
            sq = es.enter_context(nc.sbuf_tensor("nt_sq" + tag, [128, 1024], F32))
            st = es.enter_context(nc.sbuf_tensor("nt_st" + tag, [128, 4], F32))
            hn = [es.enter_context(nc.sbuf_tensor("nt_hn%d%s" % (i, tag), [128, 1024], BF16)) for i in range(2)]
            B_sq, B_st = Buf(), Buf()
            B_hn = [Buf(), Buf()]
            cnt = [0]

            def pre(xb, B_x, gb, B_g):
                i = cnt[0] % 2
                cnt[0] += 1
                k.op("act", lambda e: e.activation(out=sq[:], in_=xb, func=AF.Square), [B_x], [B_sq])
                k.op("dve", lambda e: e.tensor_reduce(out=st[:, 0:1], in_=sq[:], op=ALU.add, axis=AX.X), [B_sq], [B_st])
                k.op("act", lambda e: e.activation(out=st[:, 1:2], in_=st[:, 0:1], func=AF.Ln, bias=C_EPS, scale=1.0 / D), [B_st, B_const], [B_st])
                k.op("act", lambda e: e.activation(out=st[:, 2:3], in_=st[:, 1:2], func=AF.Exp, bias=C_ZERO, scale=-0.5), [B_st, B_const], [B_st])
                k.op("dve", lambda e: e.scalar_tensor_tensor(out=hn[i][:], in0=xb, scalar=st[:, 2:3], in1=gb[:], op0=ALU.mult, op1=ALU.mult),
                     [B_x, B_st, B_g], [B_hn[i]])
                return i

            def post(i, hT_out, B_hT, pT, B_pT):
                for c in range(8):
                    k.op("pe", lambda e, c=c: e.transpose(pT[:, c, :], hn[i][:, c * 128:(c + 1) * 128], ident[:]), [B_hn[i], B_const], [B_pT])
                k.op("dve", lambda e: e.tensor_copy(out=hT_out, in_=pT[:]), [B_pT], [B_hT])

            def fn(xb, B_x, gb, B_g, hT_out, B_hT, pT, B_pT):
                i = pre(xb, B_x, gb, B_g)
                post(i, hT_out, B_hT, pT, B_pT)

            fn.pre = pre
            fn.post = post
            return fn

        def headnorm_factory(es, tag, N):
            sqs = Rot([(es.enter_context(nc.sbuf_tensor("hn_sq%d%s" % (i, tag), [128, N], BF16)), Buf()) for i in range(4)])
            lns = Rot([(es.enter_context(nc.sbuf_tensor("hn_ln%d%s" % (i, tag), [128, N], F32)), Buf()) for i in range(2)])
            rss = Rot([(es.enter_context(nc.sbuf_tensor("hn_rs%d%s" % (i, tag), [128, N], F32)), Buf()) for i in range(2)])
            qns = Rot([(es.enter_context(nc.sbuf_tensor("hn_qn%d%s" % (i, tag), [128, N], BF16)), Buf()) for i in range(2)])
            tts = Rot([(es.enter_context(nc.sbuf_tensor("hn_tt%d%s" % (i, tag), [128, N], F32)), Buf()) for i in range(2)])
            uus = Rot([(es.enter_context(nc.sbuf_tensor("hn_uu%d%s" % (i, tag), [128, N], F32)), Buf()) for i in range(2)])

            def stats(P, srcs, ones_ap, dh, lnscale, pss, B_pss):
                for j, (sap, B_s) in enumerate(srcs):
                    sq, B_sq = sqs.next()
                    k.op("act", lambda e, sq=sq, sap=sap: e.activation(out=sq[0:P, :], in_=sap, func=AF.Square), [B_s], [B_sq])
                    k.op("pe", lambda e, sq=sq, j=j: e.matmul(pss[0:P, :], lhsT=ones_ap, rhs=sq[0:P, :], start=(j == 0), stop=(j == len(srcs) - 1)),
                         [B_sq, B_const], [B_pss])
                ln, B_ln = lns.next()
                rs, B_rs = rss.next()
                k.op("act", lambda e: e.activation(out=ln[0:P, :], in_=pss[0:P, :], func=AF.Ln, bias=C_EPS[0:P, :], scale=1.0 / dh), [B_pss, B_const], [B_ln])
                k.op("act", lambda e: e.activation(out=rs[0:P, :], in_=ln[0:P, :], func=AF.Exp, bias=lnscale[0:P, :], scale=-0.5), [B_ln, B_const], [B_rs])
                return rs, B_rs

            def stats_sq(P, srcs):
                sql = []
                for (sap, B_s) in srcs:
                    sq, B_sq = sqs.next()
                    k.op("act", lambda e, sq=sq, sap=sap: e.activation(out=sq[0:P, :], in_=sap, func=AF.Square), [B_s], [B_sq])
                    sql.append((sq, B_sq))
                return sql

            def stats_fin(P, sql, ones_ap, dh, lnscale, pss, B_pss):
                for j, (sq, B_sq) in enumerate(sql):
                    k.op("pe", lambda e, sq=sq, j=j: e.matmul(pss[0:P, :], lhsT=ones_ap, rhs=sq[0:P, :], start=(j == 0), stop=(j == len(sql) - 1)),
                         [B_sq, B_const], [B_pss])
                ln, B_ln = lns.next()
                rs, B_rs = rss.next()
                k.op("act", lambda e: e.activation(out=ln[0:P, :], in_=pss[0:P, :], func=AF.Ln, bias=C_EPS[0:P, :], scale=1.0 / dh), [B_pss, B_const], [B_ln])
                k.op("act", lambda e: e.activation(out=rs[0:P, :], in_=ln[0:P, :], func=AF.Exp, bias=lnscale[0:P, :], scale=-0.5), [B_ln, B_const], [B_rs])
                return rs, B_rs

            def rope_a(P, qn, B_qn, r_ap, ct, st_, B_tab, prq, B_prq):
                tt, B_tt = tts.next()
                uu, B_uu = uus.next()
                k.op("pe", lambda e: e.matmul(prq[0:P, :], lhsT=r_ap, rhs=qn[0:P, :], start=True, stop=True), [B_qn, B_const], [B_prq])
                k.op("pool", lambda e: e.tensor_tensor(out=tt[0:P, :], in0=qn[0:P, :], in1=ct, op=ALU.mult), [B_qn, B_tab], [B_tt])
                k.op("dve", lambda e: e.tensor_tensor(out=uu[0:P, :], in0=prq[0:P, :], in1=st_, op=ALU.mult), [B_prq, B_tab], [B_uu])
                return tt, B_tt, uu, B_uu

            def rope_b(P, tt, B_tt, uu, B_uu, out_ap, B_out):
                k.op("pool", lambda e: e.tensor_tensor(out=out_ap, in0=tt[0:P, :], in1=uu[0:P, :], op=ALU.add), [B_tt, B_uu], [B_out])

            def apply(P, sap, B_s, gcol, B_g, rs, B_rs, out_ap, B_out):
                k.op("dve", lambda e: e.scalar_tensor_tensor(out=out_ap, in0=sap, scalar=gcol, in1=rs[0:P, :], op0=ALU.mult, op1=ALU.mult),
                     [B_s, B_g, B_rs], [B_out])

            def rope(P, qn, B_qn, r_ap, ct, st_, B_tab, prq, B_prq, out_ap, B_out):
                tt, B_tt = tts.next()
                uu, B_uu = uus.next()
                k.op("pe", lambda e: e.matmul(prq[0:P, :], lhsT=r_ap, rhs=qn[0:P, :], start=True, stop=True), [B_qn, B_const], [B_prq])
                k.op("pool", lambda e: e.tensor_tensor(out=tt[0:P, :], in0=qn[0:P, :], in1=ct, op=ALU.mult), [B_qn, B_tab], [B_tt])
                k.op("dve", lambda e: e.tensor_tensor(out=uu[0:P, :], in0=prq[0:P, :], in1=st_, op=ALU.mult), [B_prq, B_tab], [B_uu])
                k.op("pool", lambda e: e.tensor_tensor(out=out_ap, in0=tt[0:P, :], in1=uu[0:P, :], op=ALU.add), [B_tt, B_uu], [B_out])

            stats.sq = stats_sq
            stats.fin = stats_fin
            rope.a = rope_a
            rope.b = rope_b
            return stats, apply, rope, qns

        def load_gcol(t_ap, src_1d, B_g, n):
            k.dma("sp", t_ap, src_1d.rearrange("(p o) -> p o", o=1), [], [B_g], sem_ld[0])

        LOOK = 3

        def attn_do_pv(es_bufs, tile, grp, pt, B_pt):
            (ps_s, ps_o, pts, rdt, osbs, ost, state) = es_bufs
            po, B_po, nk = tile["po"], tile["B_po"], tile["nk"]
            for u, j in enumerate(grp):
                idx = tile["ndone"]
                tile["ndone"] += 1
                k.op("pe", lambda e, u=u, j=j, idx=idx: e.matmul(po[0:65, :], lhsT=tile["v_ap_fn"](j), rhs=pt[:, u * 512:(u + 1) * 512], start=(idx == 0), stop=(idx == nk - 1)),
                     [B_pt] + tile["B_in"], [B_po])
            for d in state["epi"]:
                d[0] -= 1
            while state["epi"] and state["epi"][0][0] <= 0:
                state["epi"].pop(0)[1]()
            if tile["ndone"] == nk:
                osb, B_osb = osbs.next()
                k.op("dve", lambda e: e.tensor_copy(out=osb[0:65, :], in_=po[0:65, :]), [B_po], [B_osb])
                rd, B_rd = rdt.next()

                def part1b():
                    if state["recip_dve"]:
                        k.op("dve", lambda e: e.tensor_tensor(out=rd[0:1, 0:512], in0=osb[0:1, :], in1=tinyrow[0:1, :], op=ALU.max), [B_osb, B_const], [B_rd])
                        k.op("dve", lambda e: e.reciprocal(out=rd[0:1, 512:1024], in_=rd[0:1, 0:512]), [B_rd], [B_rd])
                    else:
                        k.op("act", lambda e: e.activation(out=rd[0:1, 0:512], in_=osb[0:1, :], func=AF.Ln, bias=C_TINY[0:1, :], scale=1.0), [B_osb, B_const], [B_rd])
                        k.op("act", lambda e: e.activation(out=rd[0:1, 512:1024], in_=rd[0:1, 0:512], func=AF.Exp, bias=C_ZERO[0:1, :], scale=-1.0), [B_rd, B_const], [B_rd])

                def part2():
                    k.op("pe", lambda e: e.matmul(po[0:65, :], lhsT=ones_f[0:1, 0:65], rhs=rd[0:1, 512:1024], start=True, stop=True), [B_rd, B_const], [B_po])
                    o, B_o = ost.next()
                    k.op("dve", lambda e: e.tensor_tensor(out=o[0:65, :], in0=osb[0:65, :], in1=po[0:65, :], op=ALU.mult), [B_osb, B_po], [B_o])
                    k.dma("sp", tile["out"], o[1:65, :], [B_o], [tile["B_outd"]], sem_st, nowaw=True)

                state["epi"].append([1, part1b])
                state["epi"].append([3, part2])

        def attention_tile(es_bufs, qk_rows, q_ap, k_ap_fn, v_ap_fn, kblocks, mask_fn, out_dram_ap, B_in, B_outd, engines_mask="dve"):
            (ps_s, ps_o, pts, rdt, osbs, ost, state) = es_bufs
            nk = len(kblocks)
            groups = [kblocks[i:i + 2] for i in range(0, nk, 2)]
            po, B_po = ps_o.next()
            tile = dict(po=po, B_po=B_po, nk=nk, ndone=0, v_ap_fn=v_ap_fn, B_in=B_in, out=out_dram_ap, B_outd=B_outd)
            for grp in groups:
                pS, B_pS = ps_s.next()
                for u, j in enumerate(grp):
                    k.op("pe", lambda e, u=u, j=j, pS=pS: e.matmul(pS[:, u * 512:(u + 1) * 512], lhsT=k_ap_fn(j), rhs=q_ap, start=True, stop=True), B_in, [B_pS])
                w = 512 * len(grp)
                pt, B_pt = pts.next()
                k.op("act", lambda e, pS=pS, pt=pt, w=w: e.activation(out=pt[:, 0:w], in_=pS[:, 0:w], func=AF.Exp), [B_pS], [B_pt])
                if mask_fn is not None:
                    m_ap = mask_fn(grp)
                    k.op(engines_mask, lambda e, pt=pt, m_ap=m_ap, w=w, n=len(grp): e.tensor_tensor(out=pt[:, 0:w].rearrange("p (u n) -> p u n", u=n),
                                                                                             in0=pt[:, 0:w].rearrange("p (u n) -> p u n", u=n), in1=m_ap, op=ALU.mult),
                         [B_pt, B_const], [B_pt])
                state["pend"].append((tile, grp, pt, B_pt))
                if len(state["pend"]) > LOOK:
                    attn_do_pv(es_bufs, *state["pend"].pop(0))

        def attention_flush(es_bufs):
            state = es_bufs[-1]
            while state["pend"]:
                attn_do_pv(es_bufs, *state["pend"].pop(0))
            while state["epi"]:
                state["epi"].pop(0)[1]()

        def attn_bufs(es, tag, n_s, recip_dve=False):
            ps_s = Rot([(es.enter_context(nc.psum_tensor("ps_s%d%s" % (i, tag), [128, 1024], F32)), Buf()) for i in range(3)])
            ps_o = Rot([(es.enter_context(nc.psum_tensor("ps_o%d%s" % (i, tag), [128, 512], F32)), Buf()) for i in range(2)])
            pts = Rot([(es.enter_context(nc.sbuf_tensor("pt%d%s" % (i, tag), [128, 1024], BF16)), Buf()) for i in range(6)])
            rdt = Rot([(es.enter_context(nc.sbuf_tensor("rd%d%s" % (i, tag), [1, 1024], F32)), Buf()) for i in range(3)])
            osbs = Rot([(es.enter_context(nc.sbuf_tensor("osb%d%s" % (i, tag), [128, 512], F32)), Buf()) for i in range(3)])
            ost = Rot([(es.enter_context(nc.sbuf_tensor("ost%d%s" % (i, tag), [128, 512], F32)), Buf()) for i in range(2)])
            return (ps_s, ps_o, pts, rdt, osbs, ost, {"pend": [], "epi": [], "recip_dve": recip_dve})

        for l in range(nlayers):
            x_src = x_in if l == 0 else y
            B_xsrc = B_XIN if l == 0 else B_Y

            es_l = ExitStack()
            kmT = es_l.enter_context(nc.sbuf_tensor("kmT%d" % l, [128, 4, 256], BF16))
            vm = es_l.enter_context(nc.sbuf_tensor("vm%d" % l, [128, 2, 512], BF16))
            B_kmT, B_vm = Buf(), Buf()
            with ExitStack() as es:
                def sb(name, shape, dt=F32):
                    return es.enter_context(nc.sbuf_tensor(name + "_p0_%d" % l, shape, dt))

                def ps(name, shape, dt=F32):
                    return es.enter_context(nc.psum_tensor(name + "_p0_%d" % l, shape, dt))

                wkv = sb("wkv", [128, 8, 1024], BF16)
                B_w = Buf()
                load_weights([(wkv[:, 4 * c:4 * c + 4, :], W["m_wkv"][l][c * 512:(c + 1) * 512, :].rearrange("(c p) n -> p c n", p=128), 128, 4096, 4) for c in range(2)])
                gb = sb("gb", [128, 1024])
                B_g = Buf()
                k.dma("sp", gb[:], W["mem_kv_norm_g"][l].partition_broadcast(128), [], [B_g], sem_ld[0])
                gk = sb("gk", [128, 1])
                load_gcol(gk[:, 0:1], W["m_k_norm_g"][l], B_g, 128)
                mt = sb("mt", [128, 2, 1024])
                B_mt = Buf()
                k.dma("sp", mt[:], mem_in.rearrange("(n p) d -> p n d", p=128), [], [B_mt], sem_ld[1])
                mmT = sb("mmT", [128, 8, 256], BF16)
                B_mmT = Buf()
                pT = ps("pT", [128, 8, 128], BF16)
                B_pT = Buf()
                nt = norm_transpose(es, "p0_%d" % l)
                for b in range(2):
                    nt(mt[:, b, :], B_mt, gb, B_g, mmT[:, :, b * 128:(b + 1) * 128], B_mmT, pT, B_pT)
                stats, apply, rope, qns = headnorm_factory(es, "p0_%d" % l, 256)
                pp = Rot([(ps("pp%d" % i, [128, 512]), Buf()) for i in range(2)])
                pss = Rot([(ps("pss%d" % i, [128, 512]), Buf()) for i in range(2)])
                for h in range(4):
                    p_, B_p = pp.next()
                    for c in range(8):
                        k.op("pe", lambda e, c=c, p_=p_: e.matmul(p_[:, 0:256], lhsT=wkv[:, c, h * 128:(h + 1) * 128], rhs=mmT[:, c, :], start=(c == 0), stop=(c == 7)),
                             [B_w, B_mmT], [B_p])
                    s_, B_s = pss.next()
                    rs, B_rs = stats(128, [(p_[:, 0:256], B_p)], ones_b[:, :], 128.0, C_ZERO, s_[:, 0:256], B_s)
                    apply(128, p_[:, 0:256], B_p, gk[:, 0:1], B_g, rs, B_rs, kmT[:, h, :], B_kmT)
                for b in range(2):
                    p_, B_p = pp.next()
                    for c in range(8):
                        k.op("pe", lambda e, c=c, p_=p_: e.matmul(p_[:], lhsT=mmT[:, c, b * 128:(b + 1) * 128], rhs=wkv[:, c, 512:1024], start=(c == 0), stop=(c == 7)),
                             [B_w, B_mmT], [B_p])
                    k.op("act", lambda e, p_=p_, b=b: e.activation(out=vm[:, b, :], in_=p_[:], func=AF.Copy), [B_p], [B_vm])

            if phases is not None and "p1" not in phases:
                es_l.close()
                continue
            k.barrier()
            with ExitStack() as es:
                def sb(name, shape, dt=F32):
                    return es.enter_context(nc.sbuf_tensor(name + "_p1_%d" % l, shape, dt))

                def ps(name, shape, dt=F32):
                    return es.enter_context(nc.psum_tensor(name + "_p1_%d" % l, shape, dt))

                win = sb("win", [128, 8, IN_COLS], BF16)
                wuq = sb("wuq", [128, 2, 768], BF16)
                wukv = sb("wukv", [128, 1024], BF16)
                B_w = Buf()
                load_weights([(win[:, 2 * c:2 * c + 2, :], W["w_in"][l][c * 256:(c + 1) * 256, :].rearrange("(c p) n -> p c n", p=128), 128, 2 * IN_COLS, 2) for c in range(4)]
                             + [(wuq[:, :, :], W["w_uq"][l].rearrange("(c p) n -> p c n", p=128), 128, 2 * 768, 2)]
                             + [(wukv[:], W["w_ukv"][l], 128, 1024)])
                gb = sb("gb", [128, 1024])
                gc = sb("gc", [128, 8])
                B_g = Buf()
                k.dma("sp", gb[:], W["norm_mix_g"][l].partition_broadcast(128), [], [B_g], sem_ld[0])
                load_gcol(gc[0:64, 0:1], W["a_q_norm_g"][l], B_g, 64)
                load_gcol(gc[64:128, 0:1], W["a_q_norm_g"][l], B_g, 64)
                load_gcol(gc[0:64, 1:2], W["a_k_norm_g"][l], B_g, 64)
                load_gcol(gc[64:128, 1:2], W["a_k_norm_g"][l], B_g, 64)
                load_gcol(gc[:, 2:3], W["b_cq_norm_g"][l][0:128], B_g, 128)
                load_gcol(gc[:, 3:4], W["b_cq_norm_g"][l][128:256], B_g, 128)
                load_gcol(gc[:, 4:5], W["b_ckv_norm_g"][l], B_g, 128)
                load_gcol(gc[0:96, 5:6], W["b_q_norm_g"][l], B_g, 96)
                load_gcol(gc[0:96, 6:7], W["b_k_norm_g"][l], B_g, 96)
                onesv = sb("onesv", [128, 512], BF16)
                k.op("pool", lambda e: e.memset(onesv[:], 1.0), [], [B_g])
                wk96 = sb("wk96", [128, 8, 96], BF16)
                k.op("pool", lambda e: e.memset(wk96[:], 0.0), [], [B_w])
                k.op("pool", lambda e: e.tensor_copy(out=wk96[:, :, 0:64], in_=wukv[:].rearrange("p (h two d) -> p h two d", h=8, two=2)[:, :, 0, :]), [B_w], [B_w])

                xts = Rot([(sb("xt%d" % i, [128, 4, 1024]), Buf()) for i in range(2)])
                tabs = Rot([(sb("tab%d" % i, [128, 4, 512]), Buf()) for i in range(2)])
                hTs = Rot([(sb("hT%d" % i, [128, 8, 512], BF16), Buf()) for i in range(2)])
                cqn = sb("cqn", [128, 2, 512], BF16)
                ckvn = sb("ckvn", [128, 512], BF16)
                krT = sb("krT", [32, 512], BF16)
                B_cqn, B_ckvn, B_krT = Buf(), Buf(), Buf()
                outs = Rot([(sb("qo%d" % i, [128, 512], BF16), Buf()) for i in range(3)])
                vst = Rot([(sb("vst%d" % i, [128, 8, 65], BF16), Buf()) for i in range(2)])
                pT = ps("pT", [128, 8, 128], BF16)
                B_pT = Buf()
                pp = Rot([(ps("pp%d" % i, [128, 512]), Buf()) for i in range(3)])
                pss = Rot([(ps("pss%d" % i, [128, 512]), Buf()) for i in range(2)])
                prq = Rot([(ps("prq%d" % i, [128, 512]), Buf()) for i in range(2)])
                nt = norm_transpose(es, "p1_%d" % l)
                stats, apply, rope, qns = headnorm_factory(es, "p1_%d" % l, 512)
                x_t = x_src.rearrange("(n p) d -> p n d", p=128)

                def proj(col0, ncols, hT, B_hT, p_, B_p):
                    for c in range(8):
                        k.op("pe", lambda e, c=c: e.matmul(p_[0:ncols, :], lhsT=win[:, c, col0:col0 + ncols], rhs=hT[:, c, :], start=(c == 0), stop=(c == 7)),
                             [B_w, B_hT], [B_p])

                def v_evac(p_, B_p, blk, dst, B_dst):
                    vs, B_vs = vst.next()
                    k.op("dve", lambda e: e.scalar_tensor_tensor(out=vs[:, :, 1:65], in0=p_[:].rearrange("p (h d) -> p h d", h=8), scalar=kmask[:, blk:blk + 1],
                                                                  in1=onesv[:].rearrange("p (h d) -> p h d", h=8), op0=ALU.mult, op1=ALU.mult),
                         [B_p, B_const, B_g], [B_vs])
                    k.op("dve", lambda e: e.scalar_tensor_tensor(out=vs[:, :, 0:1], in0=onesv[:, 0:8].rearrange("p (h o) -> p h o", o=1), scalar=kmask[:, blk:blk + 1],
                                                                  in1=onesv[:, 0:8].rearrange("p (h o) -> p h o", o=1), op0=ALU.mult, op1=ALU.mult), [B_const, B_g], [B_vs])
                    k.dma("sp", dst[blk * 128:(blk + 1) * 128, :], vs[:].rearrange("p h d -> p (h d)"), [B_vs], [B_dst], sem_st, nowaw=True)

                def p1_load(t):
                    xt, B_xt = xts.next()
                    k.dma("sp", xt[:], x_t[:, t * 4:(t + 1) * 4, :], B_xsrc[2 * t:2 * t + 2], [B_xt], sem_ld[1])
                    tab, B_tab = tabs.next()
                    for i_, n_ in enumerate(["c_ropeA_c", "c_ropeA_s", "c_ropeB_c", "c_ropeB_s"]):
                        k.dma("sp", tab[:, i_, :], C[n_][:, t * 512:(t + 1) * 512], [], [B_tab], sem_ld[2])
                    return xt, B_xt, tab, B_tab

                nxt = p1_load(0)
                def run_pipe(gens):
                    active = []
                    for g in list(gens) + [None] * 6:
                        for a_ in reversed(list(active)):
                            try:
                                next(a_)
                            except StopIteration:
                                active.remove(a_)
                        if g is not None:
                            try:
                                next(g)
                                active.append(g)
                            except StopIteration:
                                pass

                def g_apair(isk, pr, hT, B_hT, tab, B_tab, sl):
                    p_, B_p = pp.next()
                    proj(isk * 512 + pr * 128, 128, hT, B_hT, p_, B_p)
                    yield
                    sql = stats.sq(128, [(p_[:], B_p)])
                    yield
                    s_, B_s = pss.next()
                    rs, B_rs = stats.fin(128, sql, blk2[:, :], 64.0, C_ZERO if isk else C_L64, s_, B_s)
                    qn, B_qn = qns.next()
                    apply(128, p_[:], B_p, gc[:, isk:isk + 1], B_g, rs, B_rs, qn[:], B_qn)
                    yield
                    r_, B_r = prq.next()
                    tu = rope.a(128, qn, B_qn, ra[:, :], tab[:, 0, :], tab[:, 1, :], B_tab, r_, B_r)
                    yield
                    o_, B_o = outs.next()
                    rope.b(128, *tu, o_[:], B_o)
                    dst, B_dst = (KA, B_KA) if isk else (QA, B_QA)
                    k.dma("sp", dst[pr, :, sl], o_[:], [B_o], [B_dst], sem_st, nowaw=True)

                def g_va(b, blk, hT, B_hT):
                    p_, B_p = pp.next()
                    for c in range(8):
                        k.op("pe", lambda e, c=c, p_=p_: e.matmul(p_[:], lhsT=hT[:, c, b * 128:(b + 1) * 128], rhs=win[:, c, 1024:1536], start=(c == 0), stop=(c == 7)),
                             [B_w, B_hT], [B_p])
                    yield
                    v_evac(p_, B_p, blk, VA, B_VA)

                def g_cq(hT, B_hT):
                    p0, B_p0 = pp.next()
                    proj(1536, 128, hT, B_hT, p0, B_p0)
                    p1, B_p1 = pp.next()
                    proj(1664, 128, hT, B_hT, p1, B_p1)
                    yield
                    sql = stats.sq(128, [(p0[:], B_p0), (p1[:], B_p1)])
                    yield
                    s_, B_s = pss.next()
                    rs, B_rs = stats.fin(128, sql, ones_b[:, :], 256.0, C_ZERO, s_, B_s)
                    apply(128, p0[:], B_p0, gc[:, 2:3], B_g, rs, B_rs, cqn[:, 0, :], B_cqn)
                    apply(128, p1[:], B_p1, gc[:, 3:4], B_g, rs, B_rs, cqn[:, 1, :], B_cqn)

                def g_ckv(hT, B_hT):
                    p_, B_p = pp.next()
                    proj(1792, 128, hT, B_hT, p_, B_p)
                    yield
                    sql = stats.sq(128, [(p_[:], B_p)])
                    yield
                    s_, B_s = pss.next()
                    rs, B_rs = stats.fin(128, sql, ones_b[:, :], 128.0, C_ZERO, s_, B_s)
                    apply(128, p_[:], B_p, gc[:, 4:5], B_g, rs, B_rs, ckvn[:], B_ckvn)

                def g_kr(hT, B_hT):
                    p_, B_p = pp.next()
                    proj(1920, 32, hT, B_hT, p_, B_p)
                    yield
                    k.op("act", lambda e, p_=p_: e.activation(out=krT[:], in_=p_[0:32, :], func=AF.Copy), [B_p], [B_krT])

                def g_bhead(isk, h, tab, B_tab, sl):
                    p_, B_p = pp.next()
                    if isk:
                        k.op("pe", lambda e, p_=p_: e.matmul(p_[0:96, :], lhsT=esel[0:32, 0:96], rhs=krT[:], start=True, stop=False), [B_const, B_krT], [B_p])
                        k.op("pe", lambda e, p_=p_: e.matmul(p_[0:96, :], lhsT=wk96[:, h, :], rhs=ckvn[:], start=False, stop=True), [B_w, B_ckvn], [B_p])
                    else:
                        for j in range(2):
                            k.op("pe", lambda e, j=j, p_=p_: e.matmul(p_[0:96, :], lhsT=wuq[:, j, h * 96:(h + 1) * 96], rhs=cqn[:, j, :], start=(j == 0), stop=(j == 1)),
                                 [B_w, B_cqn], [B_p])
                    yield
                    sql = stats.sq(96, [(p_[0:96, :], B_p)])
                    yield
                    s_, B_s = pss.next()
                    rs, B_rs = stats.fin(96, sql, ones_b[0:96, 0:96], 96.0, C_ZERO if isk else C_L96, s_, B_s)
                    qn, B_qn = qns.next()
                    apply(96, p_[0:96, :], B_p, gc[0:96, 5 + isk:6 + isk], B_g, rs, B_rs, qn[0:96, :], B_qn)
                    yield
                    r_, B_r = prq.next()
                    tu = rope.a(96, qn, B_qn, rb[0:96, 0:96], tab[0:96, 2, :], tab[0:96, 3, :], B_tab, r_, B_r)
                    yield
                    o_, B_o = outs.next()
                    rope.b(96, *tu, o_[0:96, :], B_o)
                    dst, B_dst = (KBd, B_KB) if isk else (QB, B_QB)
                    k.dma("sp", dst[h, :, sl], o_[0:96, :], [B_o], [B_dst], sem_st, nowaw=True)

                wv = wukv[:].rearrange("p (h two d) -> p h two d", h=8, two=2)[:, :, 1, :]

                def g_vb(b, blk):
                    p_, B_p = pp.next()
                    k.op("pe", lambda e, p_=p_: e.matmul(p_[:], lhsT=ckvn[:, b * 128:(b + 1) * 128], rhs=wv, start=True, stop=True), [B_w, B_ckvn], [B_p])
                    yield
                    v_evac(p_, B_p, blk, VB, B_VB)

                def g_nt(b, xt_n, B_xt_n, hT_n, B_hT_n):
                    i = nt.pre(xt_n[:, b, :], B_xt_n, gb, B_g)
                    yield
                    yield
                    nt.post(i, hT_n[:, :, b * 128:(b + 1) * 128], B_hT_n, pT, B_pT)

                hT, B_hT = hTs.next()
                for b in range(4):
                    nt(nxt[0][:, b, :], nxt[1], gb, B_g, hT[:, :, b * 128:(b + 1) * 128], B_hT, pT, B_pT)
                for t in range(NT):
                    xt, B_xt, tab, B_tab = nxt
                    if t + 1 < NT:
                        nxt = p1_load(t + 1)
                        hT_n, B_hT_n = hTs.next()
                    sl = slice(t * 512, (t + 1) * 512)
                    gens = [g_cq(hT, B_hT), g_ckv(hT, B_hT), g_kr(hT, B_hT)]
                    gens += [g_apair(isk, pr, hT, B_hT, tab, B_tab, sl) for isk in range(2) for pr in range(4)]
                    gens += [g_va(b, t * 4 + b, hT, B_hT) for b in range(4)]
                    for h in range(8):
                        gens.append(g_bhead(0, h, tab, B_tab, sl))
                        if t + 1 < NT and h % 2 == 0:
                            gens.append(g_nt(h // 2, nxt[0], nxt[1], hT_n, B_hT_n))
                    gens += [g_bhead(1, h, tab, B_tab, sl) for h in range(8)]
                    gens += [g_vb(b, t * 4 + b) for b in range(4)]
                    run_pipe(gens)
                    if t + 1 < NT:
                        hT, B_hT = hT_n, B_hT_n

            if phases is not None and "p2" not in phases:
                es_l.close()
                continue
            k.barrier()
            with ExitStack() as es:
                def sb(name, shape, dt=F32):
                    return es.enter_context(nc.sbuf_tensor(name + "_p2_%d" % l, shape, dt))

                vall = sb("vall", [128, NB, 8 * 65], BF16)
                B_v = Buf()
                k.dma("sp", vall[:], VB.rearrange("(n p) f -> p n f", p=128), [B_VB], [B_v], sem_ld[3])
                qs = Rot([(sb("q%d" % i, [96, S], BF16), Buf()) for i in range(2)])
                ks = Rot([(sb("k%d" % i, [96, S], BF16), Buf()) for i in range(2)])
                bufs = attn_bufs(es, "_p2_%d" % l, 3, recip_dve=False)
                def p2_load(h):
                    q_, B_q = qs.next()
                    k_, B_k = ks.next()
                    k.dma("sp", q_[:], QB[h], [B_QB], [B_q], sem_ld[4])
                    k.dma("sp", k_[:], KBd[h], [B_KB], [B_k], sem_ld[5])
                    return q_, B_q, k_, B_k

                nxt = p2_load(0)
                for h in range(8):
                    q_, B_q, k_, B_k = nxt
                    if h + 1 < 8:
                        nxt = p2_load(h + 1)
                    for t in range(NT):
                        attention_tile(bufs, 96, q_[:, t * 512:(t + 1) * 512],
                                       lambda j, k_=k_: k_[:, j * 128:(j + 1) * 128],
                                       lambda j, h=h: vall[:, j, h * 65:(h + 1) * 65],
                                       list(range(NB)), None,
                                       OT[4 + h // 2, (h % 2) * 64:(h % 2) * 64 + 64, t * 512:(t + 1) * 512],
                                       [B_q, B_k, B_v], B_OT)

                attention_flush(bufs)
            k.barrier()
            with ExitStack() as es:
                def sb(name, shape, dt=F32):
                    return es.enter_context(nc.sbuf_tensor(name + "_p3_%d" % l, shape, dt))

                vall = sb("vall", [128, NB, 8 * 65], BF16)
                B_v = Buf()
                k.dma("sp", vall[:], VA.rearrange("(n p) f -> p n f", p=128), [B_VA], [B_v], sem_ld[3])
                band = sb("band", [128, 20, 512], BF16)
                B_band = Buf()
                load_weights([(band[:, n_:n_ + 5, :], C["c_band"][n_:n_ + 5].rearrange("c p n -> p c n"), 128, 2560, 5) for n_ in range(0, 20, 5)])
                qz = [sb("qz%d" % i, [128, S], BF16) for i in range(2)]
                B_qz = [Buf(), Buf()]
                k.op("pool", lambda e: e.memset(qz[0][64:128, :], 0.0), [], [B_qz[0]])
                k.op("pool", lambda e: e.memset(qz[1][0:64, :], 0.0), [], [B_qz[1]])
                ks = Rot([(sb("k%d" % i, [128, S], BF16), Buf()) for i in range(2)])
                bufs = attn_bufs(es, "_p3_%d" % l, 3)

                def p3_loadk(pr):
                    k_, B_k = ks.next()
                    k.dma("sp", k_[:], KA[pr], [B_KA], [B_k], sem_ld[5])
                    return k_, B_k

                nxt = p3_loadk(0)
                for pr in range(4):
                    k_, B_k = nxt
                    k.dma("sp", qz[0][0:64, :], QA[pr, 0:64, :], [B_QA], [B_qz[0]], sem_ld[4])
                    k.dma("sp", qz[1][64:128, :], QA[pr, 64:128, :], [B_QA], [B_qz[1]], sem_ld[4])
                    if pr + 1 < 4:
                        nxt = p3_loadk(pr + 1)
                    for hh in range(2):
                        h = pr * 2 + hh
                        rows = slice(hh * 64, hh * 64 + 64)
                        for t in range(NT):
                            kbl = [j for j in range(4 * t - 8, 4 * t + 12) if 0 <= j < NB]
                            attention_tile(bufs, 128, qz[hh][:, t * 512:(t + 1) * 512],
                                           lambda j, k_=k_: k_[:, j * 128:(j + 1) * 128],
                                           lambda j, h=h: vall[:, j, h * 65:(h + 1) * 65],
                                           kbl, lambda grp, t=t: band[:, grp[0] - 4 * t + 8:grp[0] - 4 * t + 8 + len(grp), :],
                                           OT[pr, rows, t * 512:(t + 1) * 512],
                                           [B_qz[hh], B_k, B_v, B_band], B_OT)

                attention_flush(bufs)
            k.barrier()
            with ExitStack() as es:
                def sb(name, shape, dt=F32):
                    return es.enter_context(nc.sbuf_tensor(name + "_p4_%d" % l, shape, dt))

                def ps(name, shape, dt=F32):
                    return es.enter_context(nc.psum_tensor(name + "_p4_%d" % l, shape, dt))

                wout = sb("wout", [128, 8, 1024], BF16)
                mwq = sb("mwq", [128, 8, 512], BF16)
                mwo = sb("mwo", [128, 4, 1024], BF16)
                B_w = Buf()
                load_weights([(wout[:, 4 * c:4 * c + 4, :], W["w_out"][l][c * 512:(c + 1) * 512, :].rearrange("(c p) n -> p c n", p=128), 128, 4096, 4) for c in range(2)]
                             + [(mwq[:, :, :], W["m_wq"][l].rearrange("(c p) n -> p c n", p=128), 128, 4096, 8)]
                             + [(mwo[:, :, :], W["m_wo"][l].rearrange("(c p) n -> p c n", p=128), 128, 4096, 4)])
                gb = sb("gb", [128, 1024])
                gc = sb("gc", [128, 12])
                B_g = Buf()
                k.dma("sp", gb[:], W["norm_mem_g"][l].partition_broadcast(128), [], [B_g], sem_ld[0])
                for j in range(4):
                    load_gcol(gc[:, j:j + 1], W["a_out_norm_g"][l][j * 128:(j + 1) * 128], B_g, 128)
                    load_gcol(gc[:, 4 + j:5 + j], W["b_out_norm_g"][l][j * 128:(j + 1) * 128], B_g, 128)
                load_gcol(gc[:, 8:9], W["m_q_norm_g"][l], B_g, 128)

                xts = Rot([(sb("xt%d" % i, [128, 4, 1024]), Buf()) for i in range(2)])
                ots = Rot([(sb("ot%d" % i, [128, 8, 512]), Buf()) for i in range(2)])
                mixT = sb("mixT", [128, 8, 512], BF16)
                B_mix = Buf()
                hT = sb("hT", [128, 8, 512], BF16)
                B_hT = Buf()
                omT = sb("omT", [128, 4, 512], BF16)
                B_om = Buf()
                pts = Rot([(sb("pt%d" % i, [128, 512], BF16), Buf()) for i in range(4)])
                rds = Rot([(sb("rdm%d" % i, [128, 512]), Buf()) for i in range(2)])
                pT = ps("pT", [128, 8, 128], BF16)
                B_pT = Buf()
                pp = Rot([(ps("pp%d" % i, [128, 512]), Buf()) for i in range(2)])
                pq = Rot([(ps("pq%d" % i, [128, 512]), Buf()) for i in range(2)])
                hsq = Rot([(sb("hsq%d" % i, [128, 512], BF16), Buf()) for i in range(4)])
                pss = Rot([(ps("pss%d" % i, [128, 512]), Buf()) for i in range(1)])
                pso = Rot([(ps("pso%d" % i, [128, 512]), Buf()) for i in range(2)])
                nt = norm_transpose(es, "p4_%d" % l)
                stats, apply, rope, qns = headnorm_factory(es, "p4_%d" % l, 512)
                x_t = x_src.rearrange("(n p) d -> p n d", p=128)
                y_t = y.rearrange("(n p) d -> p n d", p=128)
                def p4_load(t):
                    sl = slice(t * 512, (t + 1) * 512)
                    xt, B_xt = xts.next()
                    k.dma("sp", xt[:], x_t[:, t * 4:(t + 1) * 4, :], B_xsrc[2 * t:2 * t + 2], [B_xt], sem_ld[1])
                    ot, B_ot = ots.next()
                    k.dma("sp", ot[:], OT[:, :, sl].rearrange("n p f -> p n f"), [B_OT], [B_ot], sem_ld[2])
                    return xt, B_xt, ot, B_ot

                def h1_onorm(T):
                    ot, B_ot = T["ot"], T["B_ot"]
                    for grp in range(2):
                        s_, B_s = pss.next()
                        rs, B_rs = stats(128, [(ot[:, grp * 4 + j, :], B_ot) for j in range(4)], ones_b[:, :], 512.0, C_ZERO, s_, B_s)
                        for j in range(4):
                            apply(128, ot[:, grp * 4 + j, :], B_ot, gc[:, grp * 4 + j:grp * 4 + j + 1], B_g, rs, B_rs, mixT[:, grp * 4 + j, :], B_mix)

                def h1_wout(T, groups):
                    xt, B_xt = T["xt"], T["B_xt"]
                    for (b, hf) in groups:
                        p_, B_p = pp.next()
                        for j in range(8):
                            k.op("pe", lambda e, j=j, p_=p_: e.matmul(p_[:], lhsT=mixT[:, j, b * 128:(b + 1) * 128], rhs=wout[:, j, hf * 512:(hf + 1) * 512], start=(j == 0), stop=(j == 7)),
                                 [B_w, B_mix], [B_p])
                        k.op("dve", lambda e, p_=p_: e.tensor_tensor(out=xt[:, b, hf * 512:(hf + 1) * 512], in0=xt[:, b, hf * 512:(hf + 1) * 512], in1=p_[:], op=ALU.add),
                             [B_p, B_xt], [B_xt])

                def h1_nt(T):
                    for b in range(4):
                        nt(T["xt"][:, b, :], T["B_xt"], gb, B_g, hT[:, :, b * 128:(b + 1) * 128], B_hT, pT, B_pT)

                def hA1(h):
                    p_, B_p = pq.next()
                    for c in range(8):
                        k.op("pe", lambda e, c=c, p_=p_: e.matmul(p_[:], lhsT=mwq[:, c, h * 128:(h + 1) * 128], rhs=hT[:, c, :], start=(c == 0), stop=(c == 7)),
                             [B_w, B_hT], [B_p])
                    sq, B_sq = hsq.next()
                    k.op("act", lambda e, sq=sq, p_=p_: e.activation(out=sq[:], in_=p_[:], func=AF.Square), [B_p], [B_sq])
                    return p_, B_p, [(sq, B_sq)]

                def hA2(h, p_, B_p, sql):
                    s_, B_s = pss.next()
                    rs, B_rs = stats.fin(128, sql, ones_b[:, :], 128.0, C_L128, s_, B_s)
                    qn, B_qn = qns.next()
                    apply(128, p_[:], B_p, gc[:, 8:9], B_g, rs, B_rs, qn[:], B_qn)
                    return qn, B_qn

                def hB1(h, qn, B_qn):
                    ptl = []
                    for kb_ in range(2):
                        pS, B_pS = pp.next()
                        k.op("pe", lambda e, pS=pS, kb_=kb_: e.matmul(pS[:], lhsT=kmT[:, h, kb_ * 128:(kb_ + 1) * 128], rhs=qn[:], start=True, stop=True), [B_kmT, B_qn], [B_pS])
                        pt, B_pt = pts.next()
                        k.op("act", lambda e, pS=pS, pt=pt: e.activation(out=pt[:], in_=pS[:], func=AF.Exp), [B_pS], [B_pt])
                        ptl.append((pt, B_pt))
                    return ptl

                def hB2(h, ptl):
                    po, B_po = pso.next()
                    pd, B_pd = pso.next()
                    for kb_, (pt, B_pt) in enumerate(ptl):
                        k.op("pe", lambda e, pt=pt, kb_=kb_: e.matmul(po[:], lhsT=vm[:, kb_, h * 128:(h + 1) * 128], rhs=pt[:], start=(kb_ == 0), stop=(kb_ == 1)), [B_vm, B_pt], [B_po])
                    for kb_, (pt, B_pt) in enumerate(ptl):
                        k.op("pe", lambda e, pt=pt, kb_=kb_: e.matmul(pd[:], lhsT=ones_b[:, :], rhs=pt[:], start=(kb_ == 0), stop=(kb_ == 1)), [B_const, B_pt], [B_pd])
                    rd, B_rd = rds.next()
                    k.op("act", lambda e, rd=rd: e.activation(out=rd[:], in_=pd[:], func=AF.Ln, bias=C_TINY, scale=1.0), [B_pd, B_const], [B_rd])
                    k.op("act", lambda e, rd=rd: e.activation(out=rd[:], in_=rd[:], func=AF.Exp, bias=C_ZERO, scale=-1.0), [B_rd, B_const], [B_rd])
                    k.op("dve", lambda e, rd=rd: e.tensor_tensor(out=omT[:, h, :], in0=po[:], in1=rd[:], op=ALU.mult), [B_po, B_rd], [B_om])

                def h2_mwo(T):
                    xt, B_xt = T["xt"], T["B_xt"]
                    for b in range(4):
                        for hf in range(2):
                            p_, B_p = pp.next()
                            for j in range(4):
                                k.op("pe", lambda e, j=j, p_=p_: e.matmul(p_[:], lhsT=omT[:, j, b * 128:(b + 1) * 128], rhs=mwo[:, j, hf * 512:(hf + 1) * 512], start=(j == 0), stop=(j == 3)),
                                     [B_w, B_om], [B_p])
                            k.op("dve", lambda e, p_=p_: e.tensor_tensor(out=xt[:, b, hf * 512:(hf + 1) * 512], in0=xt[:, b, hf * 512:(hf + 1) * 512], in1=p_[:], op=ALU.add),
                                 [B_p, B_xt], [B_xt])

                def p4_tile(t):
                    xt, B_xt, ot, B_ot = p4_load(t)
                    return dict(xt=xt, B_xt=B_xt, ot=ot, B_ot=B_ot)

                WG = [(b, hf) for b in range(4) for hf in range(2)]
                T = p4_tile(0)
                h1_onorm(T)
                h1_wout(T, WG)
                h1_nt(T)
                for t in range(NT):
                    Tn = p4_tile(t + 1) if t + 1 < NT else None
                    a0 = hA1(0)
                    a1 = hA1(1)
                    q0 = hA2(0, *a0)
                    a2 = hA1(2)
                    e0 = hB1(0, *q0)
                    q1 = hA2(1, *a1)
                    a3 = hA1(3)
                    hB2(0, e0)
                    e1 = hB1(1, *q1)
                    q2 = hA2(2, *a2)
                    if Tn:
                        h1_onorm(Tn)
                    hB2(1, e1)
                    e2 = hB1(2, *q2)
                    q3 = hA2(3, *a3)
                    if Tn:
                        h1_wout(Tn, WG[0:3])
                    hB2(2, e2)
                    e3 = hB1(3, *q3)
                    if Tn:
                        h1_wout(Tn, WG[3:6])
                    hB2(3, e3)
                    if Tn:
                        h1_wout(Tn, WG[6:8])
                        h1_nt(Tn)
                    h2_mwo(T)
                    k.dma("sp", y_t[:, t * 4:(t + 1) * 4, :], T["xt"][:], [T["B_xt"]], B_Y[2 * t:2 * t + 2], sem_st)
                    T = Tn
            es_l.close()

            k.barrier()
            if phases is not None and "p5" not in phases:
                continue
            with ExitStack() as es:
                def sb(name, shape, dt=F32):
                    return es.enter_context(nc.sbuf_tensor(name + "_p5_%d" % l, shape, dt))

                def ps(name, shape, dt=F32):
                    return es.enter_context(nc.psum_tensor(name + "_p5_%d" % l, shape, dt))

                w1 = sb("w1", [128, 8, DFF], BF16)
                w2 = sb("w2", [128, 32, 1024], BF16)
                B_w = Buf()
                load_weights([(w1[:, c, :], W["w_ff1"][l][c * 128:(c + 1) * 128, :], 128, DFF) for c in range(8)]
                             + [(w2[:, 4 * c:4 * c + 4, :], W["w_ff2"][l][c * 512:(c + 1) * 512, :].rearrange("(c p) n -> p c n", p=128), 128, 4096, 4) for c in range(8)])
                gb = sb("gb", [128, 1024])
                B_g = Buf()
                k.dma("sp", gb[:], W["norm_ffn_g"][l].partition_broadcast(128), [], [B_g], sem_ld[0])
                xts = Rot([(sb("xt%d" % i, [128, 2, 1024]), Buf()) for i in range(2)])
                hT = sb("hT", [128, 8, 256], BF16)
                B_hT = Buf()
                aT = sb("aT", [128, 32, 256], BF16)
                B_aT = Buf()
                rls = Rot([(sb("rl%d" % i, [128, 512]), Buf()) for i in range(2)])
                pT = ps("pT", [128, 8, 128], BF16)
                B_pT = Buf()
                pp = Rot([(ps("pp%d" % i, [128, 512]), Buf()) for i in range(3)])
                po = Rot([(ps("po%d" % i, [128, 512]), Buf()) for i in range(3)])
                nt = norm_transpose(es, "p5_%d" % l)
                y_t = y.rearrange("(n p) d -> p n d", p=128)
                def p5_load(t):
                    xt, B_xt = xts.next()
                    k.dma("sp", xt[:], y_t[:, t * 2:(t + 1) * 2, :], [B_Y[t]], [B_xt], sem_ld[1])
                    return xt, B_xt

                nxt = p5_load(0)
                NT5 = S // 256
                for b in range(2):
                    nt(nxt[0][:, b, :], nxt[1], gb, B_g, hT[:, :, b * 128:(b + 1) * 128], B_hT, pT, B_pT)
                for t in range(NT5):
                    xt, B_xt = nxt
                    if t + 1 < NT5:
                        nxt = p5_load(t + 1)
                    for f2 in range(16):
                        p_, B_p = pp.next()
                        for u in range(2):
                            fc = f2 * 2 + u
                            for c in range(8):
                                k.op("pe", lambda e, c=c, fc=fc, u=u, p_=p_: e.matmul(p_[:, u * 256:(u + 1) * 256], lhsT=w1[:, c, fc * 128:(fc + 1) * 128], rhs=hT[:, c, :], start=(c == 0), stop=(c == 7)),
                                     [B_w, B_hT], [B_p])
                        rl, B_rl = rls.next()
                        k.op("act", lambda e, p_=p_, rl=rl: e.activation(out=rl[:], in_=p_[:], func=AF.Relu), [B_p], [B_rl])
                        k.op("pool", lambda e, rl=rl, f2=f2: e.tensor_tensor(out=aT[:, f2 * 2:f2 * 2 + 2, :], in0=rl[:].rearrange("p (u n) -> p u n", u=2),
                                                                          in1=rl[:].rearrange("p (u n) -> p u n", u=2), op=ALU.mult), [B_rl], [B_aT])
                    pre_i = []
                    if t + 1 < NT5:
                        pre_i = [nt.pre(nxt[0][:, b, :], nxt[1], gb, B_g) for b in range(2)]
                    gi = 0
                    for b in range(2):
                        for hf in range(2):
                            p_, B_p = po.next()
                            for fc in range(32):
                                k.op("pe", lambda e, fc=fc, p_=p_: e.matmul(p_[:], lhsT=aT[:, fc, b * 128:(b + 1) * 128], rhs=w2[:, fc, hf * 512:(hf + 1) * 512], start=(fc == 0), stop=(fc == 31)),
                                     [B_w, B_aT], [B_p])
                            k.op("dve", lambda e, p_=p_: e.tensor_tensor(out=xt[:, b, hf * 512:(hf + 1) * 512], in0=xt[:, b, hf * 512:(hf + 1) * 512], in1=p_[:], op=ALU.add),
                                 [B_p, B_xt], [B_xt])
                            if gi == 1:
                                for b2, i_ in enumerate(pre_i):
                                    nt.post(i_, hT[:, :, b2 * 128:(b2 + 1) * 128], B_hT, pT, B_pT)
                            gi += 1
                    k.dma("sp", y_t[:, t * 2:(t + 1) * 2, :], xt[:], [B_xt], [B_Y[t]], sem_st)
            k.barrier()

        k.barrier()
    return nc


_NC_CACHE = {}


def run_cores(xs, mems, nvalid, weights, S, debug=False, **bkw):
    key = (S, debug, tuple(sorted(bkw.items())))
    if key not in _NC_CACHE:
        _NC_CACHE[key] = build(S, debug=debug, **bkw)
    nc = _NC_CACHE[key]
    consts = host_consts(S)
    in_maps = []
    for x, m, nv in zip(xs, mems, nvalid):
        km = (np.arange(S) < nv).astype(np.float32).reshape(S // 128, 128).T.copy()
        d = {"x": np.ascontiguousarray(x, dtype=np.float32), "kmask": km, "mem": np.ascontiguousarray(m, dtype=np.float32)}
        d.update(weights)
        d.update(consts)
        in_maps.append(d)
    res = run_bass_kernel_spmd(nc, in_maps, core_ids=list(range(len(xs))))
    return res.results


def kernel(**inputs):
    weights = {n: np.ascontiguousarray(inputs[n], dtype=np.float32) for n, _ in WNAMES}
    xp = np.asarray(inputs["x_prompt"], dtype=np.float32)
    xs_ = np.asarray(inputs["x_sample"], dtype=np.float32)
    mp = np.asarray(inputs["mem_prompt"], dtype=np.float32)
    ms = np.asarray(inputs["mem_sample"], dtype=np.float32)
    S = SMAX
    xs, mems, nvalid = [], [], []
    for b in range(xp.shape[0]):
        pad = np.zeros((S, D), np.float32)
        pad[:xp.shape[1]] = xp[b]
        xs.append(pad)
        mems.append(mp[b])
        nvalid.append(xp.shape[1])
    for b in range(xs_.shape[0]):
        xs.append(xs_[b])
        mems.append(ms[b])
        nvalid.append(xs_.shape[1])
    res = run_cores(xs, mems, nvalid, weights, S)
    y_prompt = np.stack([res[b]["y"][:xp.shape[1]] for b in range(xp.shape[0])]).astype(np.float32)
    y_sample = np.stack([res[xp.shape[0] + b]["y"] for b in range(xs_.shape[0])]).astype(np.float32)
    return (y_prompt, y_sample)
```

```python
import numpy as np
import ml_dtypes
from contextlib import ExitStack
import concourse.bass as bass
import concourse.mybir as mybir
from concourse.bass_utils import run_bass_kernel_spmd

F32 = mybir.dt.float32
BF16 = mybir.dt.bfloat16
AF = mybir.ActivationFunctionType
ALU = mybir.AluOpType
AX = mybir.AxisListType

D = 1024
DEPTH = 2
NMEM = 256
EPS = 1e-6
THETA = 500000.0
IN_COLS = 1952
DFF = 4096
SMAX = 8192

WNAMES = [
    ("norm_mix_g", [DEPTH, D]), ("w_in", [DEPTH, D, IN_COLS]), ("a_q_norm_g", [DEPTH, 64]),
    ("a_k_norm_g", [DEPTH, 64]), ("b_cq_norm_g", [DEPTH, 256]), ("b_ckv_norm_g", [DEPTH, 128]),
    ("w_uq", [DEPTH, 256, 768]), ("w_ukv", [DEPTH, 128, 1024]), ("b_q_norm_g", [DEPTH, 96]),
    ("b_k_norm_g", [DEPTH, 96]), ("a_out_norm_g", [DEPTH, 512]), ("b_out_norm_g", [DEPTH, 512]),
    ("w_out", [DEPTH, D, D]), ("norm_mem_g", [DEPTH, D]), ("mem_kv_norm_g", [DEPTH, D]),
    ("m_wq", [DEPTH, D, 512]), ("m_wkv", [DEPTH, D, 1024]), ("m_q_norm_g", [DEPTH, 128]),
    ("m_k_norm_g", [DEPTH, 128]), ("m_wo", [DEPTH, 512, D]), ("norm_ffn_g", [DEPTH, D]),
    ("w_ff1", [DEPTH, D, DFF]), ("w_ff2", [DEPTH, DFF, D]),
]


class Sem:
    def __init__(self, h, is_dma):
        self.h = h
        self.n = 0
        self.is_dma = is_dma


class Buf:
    def __init__(self, name=""):
        self.name = name
        self.w = None
        self.r = {}


class KB:
    def __init__(self, nc, es):
        self.nc = nc
        self.es = es
        self.eng = {"pe": nc.tensor, "act": nc.scalar, "dve": nc.vector, "pool": nc.gpsimd, "sp": nc.sync}
        self.esem = {}
        for e in ["pe", "act", "dve", "pool"]:
            self.esem[e] = self.newsem("c_" + e, False)
        self.waited = {e: {} for e in self.eng}
        self.free_dsems = []
        self.assigned = []
        self.ndsem = 0
        self.store_marker = None

    def newsem(self, name, is_dma=True):
        h = self.es.enter_context(self.nc.semaphore(name))
        s = Sem(h, is_dma)
        if not hasattr(self, "allsems"):
            self.allsems = []
        self.allsems.append(s)
        return s

    def barrier(self):
        for e in self.eng:
            for s in self.allsems:
                if s.n > 0 and self.waited[e].get(s, 0) < s.n:
                    self.eng[e].wait_ge(s.h, s.n)
                    self.waited[e][s] = s.n
        for b in self.assigned:
            self.free_dsems.append(b.dsem)
            b.dsem = None
        self.assigned = []

    def _deps(self, e, reads, writes):
        deps = {}

        def add(s, v):
            if deps.get(s, 0) < v:
                deps[s] = v

        for b in reads:
            if b.w:
                add(*b.w)
        for b in writes:
            if b.w:
                add(*b.w)
            for s, v in b.r.items():
                add(s, v)
        for s, v in deps.items():
            if e == "pe" and s is self.esem["pe"]:
                continue
            if s.is_dma:
                v = s.n
            if self.waited[e].get(s, 0) >= v:
                continue
            self.eng[e].wait_ge(s.h, v)
            self.waited[e][s] = v

    def op(self, e, fn, reads=(), writes=()):
        self._deps(e, reads, writes)
        s = self.esem[e]
        s.n += 1
        fn(self.eng[e]).then_inc(s.h, 1)
        for b in reads:
            b.r[s] = s.n
        for b in writes:
            b.w = (s, s.n)
            b.r = {}

    def dsem(self, buf):
        if getattr(buf, "dsem", None) is None:
            if self.free_dsems:
                buf.dsem = self.free_dsems.pop()
            else:
                self.ndsem += 1
                buf.dsem = self.newsem("d%d" % self.ndsem)
            self.assigned.append(buf)
        return buf.dsem

    def dma(self, q, out, in_, reads, writes, sem, nowaw=False, **kw):
        own = reads[0] if sem is self.store_marker else writes[0]
        sem = self.dsem(own)
        if nowaw:
            for b in writes:
                b.w = None
        self._deps(q, reads, writes)
        sem.n += 16
        self.eng[q].dma_start(out=out, in_=in_, **kw).then_inc(sem.h, 16)
        for b in reads:
            b.r[sem] = sem.n
        for b in writes:
            b.w = (sem, sem.n)
            b.r = {}

    def final_wait(self, e, bufs):
        self._deps(e, bufs, ())


class Rot:
    def __init__(self, items):
        self.items = items
        self.i = 0

    def next(self):
        it = self.items[self.i % len(self.items)]
        self.i += 1
        return it


def host_consts(S):
    c = {}
    c["c_ident"] = np.eye(128, dtype=np.float32)
    c["c_ones"] = np.ones((128, 128), np.float32)
    blk = np.zeros((128, 128), np.float32)
    blk[:64, :64] = 1
    blk[64:, 64:] = 1
    c["c_blk2"] = blk
    ra = np.zeros((128, 128), np.float32)
    for base in (0, 64):
        for i in range(8):
            ra[base + i + 8, base + i] = -1.0
            ra[base + i, base + i + 8] = 1.0
    c["c_ra"] = ra
    rb = np.zeros((128, 128), np.float32)
    for i in range(16):
        rb[64 + i + 16, 64 + i] = -1.0
        rb[64 + i, 64 + i + 16] = 1.0
    c["c_rb"] = rb
    ee = np.zeros((128, 128), np.float32)
    for i in range(32):
        ee[i, 64 + i] = 1.0
    c["c_e"] = ee
    pos = np.arange(S, dtype=np.float32)

    def tables(r):
        inv = (np.float32(THETA) ** (-np.arange(0, r, 2, dtype=np.float32) / np.float32(r))).astype(np.float32)
        ang = (pos[None, :] * inv[:, None]).astype(np.float32).astype(np.float64)
        return np.cos(ang).astype(np.float32), np.sin(ang).astype(np.float32)

    ca, sa = tables(16)
    cA = np.ones((128, S), np.float32)
    sA = np.zeros((128, S), np.float32)
    for base in (0, 64):
        cA[base:base + 8] = ca
        cA[base + 8:base + 16] = ca
        sA[base:base + 8] = sa
        sA[base + 8:base + 16] = sa
    cb, sb_ = tables(32)
    cB = np.ones((128, S), np.float32)
    sB = np.zeros((128, S), np.float32)
    cB[64:80] = cb
    cB[80:96] = cb
    sB[64:80] = sb_
    sB[80:96] = sb_
    c["c_ropeA_c"], c["c_ropeA_s"], c["c_ropeB_c"], c["c_ropeB_s"] = cA, sA, cB, sB
    deltas = list(range(-1024, 1408 + 1, 128))
    kl = np.arange(128)[:, None]
    ql = np.arange(512)[None, :]
    m = np.zeros((len(deltas), 128, 512), np.float32)
    for i, dl in enumerate(deltas):
        off = dl + kl - ql
        a = np.abs(off)
        m[i] = (a <= 64).astype(np.float32) + ((a <= 256) & (off % 4 == 0)) + ((a <= 1024) & (off % 16 == 0))
    c["c_band"] = m
    return c


CONST_SHAPES = lambda S: {
    "c_ident": [128, 128], "c_ones": [128, 128], "c_blk2": [128, 128], "c_ra": [128, 128], "c_rb": [128, 128],
    "c_e": [128, 128], "c_ropeA_c": [128, S], "c_ropeA_s": [128, S], "c_ropeB_c": [128, S], "c_ropeB_s": [128, S],
    "c_band": [20, 128, 512],
}


def build(S, debug=False, nlayers=DEPTH, phases=None):
    NB = S // 128
    NT = S // 512
    nc = bass.Bass("TRN2", target_bir_lowering=False)

    def din(name, shape):
        return nc.dram_tensor(name, list(shape), F32, kind="ExternalInput").ap()

    x_in = din("x", [S, D])
    kmask_in = din("kmask", [128, NB])
    mem_in = din("mem", [NMEM, D])
    W = {n: din(n, s) for n, s in WNAMES}
    C = {n: din(n, s) for n, s in CONST_SHAPES(S).items()}
    y = nc.dram_tensor("y", [S, D], F32, kind="ExternalOutput").ap()
    skind = "ExternalOutput" if debug else "Internal"

    def dscr(name, shape, dt):
        return nc.dram_tensor(name, list(shape), dt, kind=skind).ap()

    QA = dscr("s_qa", [4, 128, S], BF16)
    KA = dscr("s_ka", [4, 128, S], BF16)
    VA = dscr("s_va", [S, 8 * 65], BF16)
    QB = dscr("s_qb", [8, 96, S], BF16)
    KBd = dscr("s_kb", [8, 96, S], BF16)
    VB = dscr("s_vb", [S, 8 * 65], BF16)
    OT = dscr("s_ot", [8, 128, S], F32)

    B_QA, B_KA, B_VA, B_QB, B_KB, B_VB, B_OT = [Buf(n) for n in ["QA", "KA", "VA", "QB", "KB", "VB", "OT"]]
    B_Y = [Buf("Y%d" % i) for i in range(S // 256)]
    B_XIN = [Buf("X%d" % i) for i in range(S // 256)]

    with ExitStack() as es_all:
        k = KB(nc, es_all)
        sem_ld = [object() for i in range(6)]
        sem_st = object()
        k.store_marker = sem_st

        def sb_p(name, shape, dt=F32):
            return es_all.enter_context(nc.sbuf_tensor(name, shape, dt))

        ident = sb_p("ident", [128, 128], BF16)
        ones_b = sb_p("ones_b", [128, 128], BF16)
        blk2 = sb_p("blk2", [128, 128], BF16)
        ra = sb_p("ra", [128, 128], BF16)
        rb = sb_p("rb", [128, 128], BF16)
        esel = sb_p("esel", [128, 128], BF16)
        ones_f = sb_p("ones_f", [128, 128], F32)
        cst = sb_p("cst", [128, 8], F32)
        kmask = sb_p("kmask_sb", [128, NB], F32)
        tinyrow = sb_p("tinyrow", [1, 512], F32)
        B_const = Buf("const")
        k.dma("sp", ones_f[:], C["c_ones"][:, :], [], [B_const], sem_ld[0])
        k.dma("sp", kmask[:], kmask_in[:, :], [], [B_const], sem_ld[0])
        for i, v in enumerate([EPS, 0.0, 1e-18, -0.5 * np.log(64.0), -0.5 * np.log(96.0), -0.5 * np.log(128.0)]):
            k.op("dve", lambda e, i=i, v=v: e.memset(cst[:, i:i + 1], float(v)), [], [B_const])
        C_EPS, C_ZERO, C_TINY, C_L64, C_L96, C_L128 = [cst[:, i:i + 1] for i in range(6)]
        k.op("dve", lambda e: e.memset(tinyrow[:], 1e-30), [], [B_const])

        uid = [0]

        def load_weights(items):
            uid[0] += 1
            with ExitStack() as es_s:
                stg = Rot([(es_s.enter_context(nc.sbuf_tensor("stg%d_%d" % (i, uid[0]), [128, 4096], F32)), Buf()) for i in range(2)])
                B_d = Buf()
                for i, it in enumerate(items):
                    dst, src_, P, n = it[0:4]
                    c = it[4] if len(it) > 4 else 1
                    s_, B_s = stg.next()
                    sv = s_[0:P, 0:n] if c == 1 else s_[0:P, 0:n].rearrange("p (c n) -> p c n", c=c)
                    k.dma("sp", sv, src_, [], [B_s], sem_ld[0])
                    k.op("dve" if i % 3 != 2 else "pool", lambda e, dst=dst, sv=sv: e.tensor_copy(out=dst, in_=sv), [B_s], [B_d])
                k.barrier()

        load_weights([(t_[:], C[n_][:, :], 128, 128) for t_, n_ in [(ident, "c_ident"), (ones_b, "c_ones"), (blk2, "c_blk2"), (ra, "c_ra"), (rb, "c_rb"), (esel, "c_e")]])

        def norm_transpose(es, tag):
            sq = es.enter_context(nc.sbuf_tensor("nt_sq" + tag, [128, 1024], F32))
            st = es.enter_context(nc.sbuf_tensor("nt_st" + tag, [128, 4], F32))
            hn = [es.enter_context(nc.sbuf_tensor("nt_hn%d%s" % (i, tag), [128, 1024], BF16)) for i in range(2)]
            B_sq, B_st = Buf(), Buf()
            B_hn = [Buf(), Buf()]
            cnt = [0]

            def pre(xb, B_x, gb, B_g):
                i = cnt[0] % 2
                cnt[0] += 1
                k.op("act", lambda e: e.activation(out=sq[:], in_=xb, func=AF.Square), [B_x], [B_sq])
                k.op("dve", lambda e: e.tensor_reduce(out=st[:, 0:1], in_=sq[:], op=ALU.add, axis=AX.X), [B_sq], [B_st])
                k.op("act", lambda e: e.activation(out=st[:, 1:2], in_=st[:, 0:1], func=AF.Ln, bias=C_EPS, scale=1.0 / D), [B_st, B_const], [B_st])
                k.op("act", lambda e: e.activation(out=st[:, 2:3], in_=st[:, 1:2], func=AF.Exp, bias=C_ZERO, scale=-0.5), [B_st, B_const], [B_st])
                k.op("dve", lambda e: e.scalar_tensor_tensor(out=hn[i][:], in0=xb, scalar=st[:, 2:3], in1=gb[:], op0=ALU.mult, op1=ALU.mult),
                     [B_x, B_st, B_g], [B_hn[i]])
                return i

            def post(i, hT_out, B_hT, pT, B_pT):
                for c in range(8):
                    k.op("pe", lambda e, c=c: e.transpose(pT[:, c, :], hn[i][:, c * 128:(c + 1) * 128], ident[:]), [B_hn[i], B_const], [B_pT])
                k.op("dve", lambda e: e.tensor_copy(out=hT_out, in_=pT[:]), [B_pT], [B_hT])

            def fn(xb, B_x, gb, B_g, hT_out, B_hT, pT, B_pT):
                i = pre(xb, B_x, gb, B_g)
                post(i, hT_out, B_hT, pT, B_pT)

            fn.pre = pre
            fn.post = post
            return fn

        def headnorm_factory(es, tag, N):
            sqs = Rot([(es.enter_context(nc.sbuf_tensor("hn_sq%d%s" % (i, tag), [128, N], BF16)), Buf()) for i in range(4)])
            lns = Rot([(es.enter_context(nc.sbuf_tensor("hn_ln%d%s" % (i, tag), [128, N], F32)), Buf()) for i in range(2)])
            rss = Rot([(es.enter_context(nc.sbuf_tensor("hn_rs%d%s" % (i, tag), [128, N], F32)), Buf()) for i in range(2)])
            qns = Rot([(es.enter_context(nc.sbuf_tensor("hn_qn%d%s" % (i, tag), [128, N], BF16)), Buf()) for i in range(2)])
            tts = Rot([(es.enter_context(nc.sbuf_tensor("hn_tt%d%s" % (i, tag), [128, N], F32)), Buf()) for i in range(2)])
            uus = Rot([(es.enter_context(nc.sbuf_tensor("hn_uu%d%s" % (i, tag), [128, N], F32)), Buf()) for i in range(2)])

            def stats(P, srcs, ones_ap, dh, lnscale, pss, B_pss):
                for j, (sap, B_s) in enumerate(srcs):
                    sq, B_sq = sqs.next()
                    k.op("act", lambda e, sq=sq, sap=sap: e.activation(out=sq[0:P, :], in_=sap, func=AF.Square), [B_s], [B_sq])
                    k.op("pe", lambda e, sq=sq, j=j: e.matmul(pss[0:P, :], lhsT=ones_ap, rhs=sq[0:P, :], start=(j == 0), stop=(j == len(srcs) - 1)),
                         [B_sq, B_const], [B_pss])
                ln, B_ln = lns.next()
                rs, B_rs = rss.next()
                k.op("act", lambda e: e.activation(out=ln[0:P, :], in_=pss[0:P, :], func=AF.Ln, bias=C_EPS[0:P, :], scale=1.0 / dh), [B_pss, B_const], [B_ln])
                k.op("act", lambda e: e.activation(out=rs[0:P, :], in_=ln[0:P, :], func=AF.Exp, bias=lnscale[0:P, :], scale=-0.5), [B_ln, B_const], [B_rs])
                return rs, B_rs

            def stats_sq(P, srcs):
                sql = []
                for (sap, B_s) in srcs:
                    sq, B_sq = sqs.next()
                    k.op("act", lambda e, sq=sq, sap=sap: e.activation(out=sq[0:P, :], in_=sap, func=AF.Square), [B_s], [B_sq])
                    sql.append((sq, B_sq))
                return sql

            def stats_fin(P, sql, ones_ap, dh, lnscale, pss, B_pss):
                for j, (sq, B_sq) in enumerate(sql):
                    k.op("pe", lambda e, sq=sq, j=j: e.matmul(pss[0:P, :], lhsT=ones_ap, rhs=sq[0:P, :], start=(j == 0), stop=(j == len(sql) - 1)),
                         [B_sq, B_const], [B_pss])
                ln, B_ln = lns.next()
                rs, B_rs = rss.next()
                k.op("act", lambda e: e.activation(out=ln[0:P, :], in_=pss[0:P, :], func=AF.Ln, bias=C_EPS[0:P, :], scale=1.0 / dh), [B_pss, B_const], [B_ln])
                k.op("act", lambda e: e.activation(out=rs[0:P, :], in_=ln[0:P, :], func=AF.Exp, bias=lnscale[0:P, :], scale=-0.5), [B_ln, B_const], [B_rs])
                return rs, B_rs

            def rope_a(P, qn, B_qn, r_ap, ct, st_, B_tab, prq, B_prq):
                tt, B_tt = tts.next()
                uu, B_uu = uus.next()
                k.op("pe", lambda e: e.matmul(prq[0:P, :], lhsT=r_ap, rhs=qn[0:P, :], start=True, stop=True), [B_qn, B_const], [B_prq])
                k.op("pool", lambda e: e.tensor_tensor(out=tt[0:P, :], in0=qn[0:P, :], in1=ct, op=ALU.mult), [B_qn, B_tab], [B_tt])
                k.op("dve", lambda e: e.tensor_tensor(out=uu[0:P, :], in0=prq[0:P, :], in1=st_, op=ALU.mult), [B_prq, B_tab], [B_uu])
                return tt, B_tt, uu, B_uu

            def rope_b(P, tt, B_tt, uu, B_uu, out_ap, B_out):
                k.op("pool", lambda e: e.tensor_tensor(out=out_ap, in0=tt[0:P, :], in1=uu[0:P, :], op=ALU.add), [B_tt, B_uu], [B_out])

            def apply(P, sap, B_s, gcol, B_g, rs, B_rs, out_ap, B_out):
                k.op("dve", lambda e: e.scalar_tensor_tensor(out=out_ap, in0=sap, scalar=gcol, in1=rs[0:P, :], op0=ALU.mult, op1=ALU.mult),
                     [B_s, B_g, B_rs], [B_out])

            def rope(P, qn, B_qn, r_ap, ct, st_, B_tab, prq, B_prq, out_ap, B_out):
                tt, B_tt = tts.next()
                uu, B_uu = uus.next()
                k.op("pe", lambda e: e.matmul(prq[0:P, :], lhsT=r_ap, rhs=qn[0:P, :], start=True, stop=True), [B_qn, B_const], [B_prq])
                k.op("pool", lambda e: e.tensor_tensor(out=tt[0:P, :], in0=qn[0:P, :], in1=ct, op=ALU.mult), [B_qn, B_tab], [B_tt])
                k.op("dve", lambda e: e.tensor_tensor(out=uu[0:P, :], in0=prq[0:P, :], in1=st_, op=ALU.mult), [B_prq, B_tab], [B_uu])
                k.op("pool", lambda e: e.tensor_tensor(out=out_ap, in0=tt[0:P, :], in1=uu[0:P, :], op=ALU.add), [B_tt, B_uu], [B_out])

            stats.sq = stats_sq
            stats.fin = stats_fin
            rope.a = rope_a
            rope.b = rope_b
            return stats, apply, rope, qns

        def load_gcol(t_ap, src_1d, B_g, n):
            k.dma("sp", t_ap, src_1d.rearrange("(p o) -> p o", o=1), [], [B_g], sem_ld[0])

        LOOK = 3

        def attn_do_pv(es_bufs, tile, grp, pt, B_pt):
            (ps_s, ps_o, pts, rdt, osbs, ost, state) = es_bufs
            po, B_po, nk = tile["po"], tile["B_po"], tile["nk"]
            for u, j in enumerate(grp):
                idx = tile["ndone"]
                tile["ndone"] += 1
                k.op("pe", lambda e, u=u, j=j, idx=idx: e.matmul(po[0:65, :], lhsT=tile["v_ap_fn"](j), rhs=pt[:, u * 512:(u + 1) * 512], start=(idx == 0), stop=(idx == nk - 1)),
                     [B_pt] + tile["B_in"], [B_po])
            for d in state["epi"]:
                d[0] -= 1
            while state["epi"] and state["epi"][0][0] <= 0:
                state["epi"].pop(0)[1]()
            if tile["ndone"] == nk:
                osb, B_osb = osbs.next()
                k.op("dve", lambda e: e.tensor_copy(out=osb[0:65, :], in_=po[0:65, :]), [B_po], [B_osb])
                rd, B_rd = rdt.next()

                def part1b():
                    if state["recip_dve"]:
                        k.op("dve", lambda e: e.tensor_tensor(out=rd[0:1, 0:512], in0=osb[0:1, :], in1=tinyrow[0:1, :], op=ALU.max), [B_osb, B_const], [B_rd])
                        k.op("dve", lambda e: e.reciprocal(out=rd[0:1, 512:1024], in_=rd[0:1, 0:512]), [B_rd], [B_rd])
                    else:
                        k.op("act", lambda e: e.activation(out=rd[0:1, 0:512], in_=osb[0:1, :], func=AF.Ln, bias=C_TINY[0:1, :], scale=1.0), [B_osb, B_const], [B_rd])
                        k.op("act", lambda e: e.activation(out=rd[0:1, 512:1024], in_=rd[0:1, 0:512], func=AF.Exp, bias=C_ZERO[0:1, :], scale=-1.0), [B_rd, B_const], [B_rd])

                def part2():
                    k.op("pe", lambda e: e.matmul(po[0:65, :], lhsT=ones_f[0:1, 0:65], rhs=rd[0:1, 512:1024], start=True, stop=True), [B_rd, B_const], [B_po])
                    o, B_o = ost.next()
                    k.op("dve", lambda e: e.tensor_tensor(out=o[0:65, :], in0=osb[0:65, :], in1=po[0:65, :], op=ALU.mult), [B_osb, B_po], [B_o])
                    k.dma("sp", tile["out"], o[1:65, :], [B_o], [tile["B_outd"]], sem_st, nowaw=True)

                state["epi"].append([1, part1b])
                state["epi"].append([3, part2])

        def attention_tile(es_bufs, qk_rows, q_ap, k_ap_fn, v_ap_fn, kblocks, mask_fn, out_dram_ap, B_in, B_outd, engines_mask="dve"):
            (ps_s, ps_o, pts, rdt, osbs, ost, state) = es_bufs
            nk = len(kblocks)
            groups = [kblocks[i:i + 2] for i in range(0, nk, 2)]
            po, B_po = ps_o.next()
            tile = dict(po=po, B_po=B_po, nk=nk, ndone=0, v_ap_fn=v_ap_fn, B_in=B_in, out=out_dram_ap, B_outd=B_outd)
            for grp in groups:
                pS, B_pS = ps_s.next()
                for u, j in enumerate(grp):
                    k.op("pe", lambda e, u=u, j=j, pS=pS: e.matmul(pS[:, u * 512:(u + 1) * 512], lhsT=k_ap_fn(j), rhs=q_ap, start=True, stop=True), B_in, [B_pS])
                w = 512 * len(grp)
                pt, B_pt = pts.next()
                k.op("act", lambda e, pS=pS, pt=pt, w=w: e.activation(out=pt[:, 0:w], in_=pS[:, 0:w], func=AF.Exp), [B_pS], [B_pt])
                if mask_fn is not None:
                    m_ap = mask_fn(grp)
                    k.op(engines_mask, lambda e, pt=pt, m_ap=m_ap, w=w, n=len(grp): e.tensor_tensor(out=pt[:, 0:w].rearrange("p (u n) -> p u n", u=n),
                                                                                             in0=pt[:, 0:w].rearrange("p (u n) -> p u n", u=n), in1=m_ap, op=ALU.mult),
                         [B_pt, B_const], [B_pt])
                state["pend"].append((tile, grp, pt, B_pt))
                if len(state["pend"]) > LOOK:
                    attn_do_pv(es_bufs, *state["pend"].pop(0))

        def attention_flush(es_bufs):
            state = es_bufs[-1]
            while state["pend"]:
                attn_do_pv(es_bufs, *state["pend"].pop(0))
            while state["epi"]:
                state["epi"].pop(0)[1]()

        def attn_bufs(es, tag, n_s, recip_dve=False):
            ps_s = Rot([(es.enter_context(nc.psum_tensor("ps_s%d%s" % (i, tag), [128, 1024], F32)), Buf()) for i in range(3)])
            ps_o = Rot([(es.enter_context(nc.psum_tensor("ps_o%d%s" % (i, tag), [128, 512], F32)), Buf()) for i in range(2)])
            pts = Rot([(es.enter_context(nc.sbuf_tensor("pt%d%s" % (i, tag), [128, 1024], BF16)), Buf()) for i in range(6)])
            rdt = Rot([(es.enter_context(nc.sbuf_tensor("rd%d%s" % (i, tag), [1, 1024], F32)), Buf()) for i in range(3)])
            osbs = Rot([(es.enter_context(nc.sbuf_tensor("osb%d%s" % (i, tag), [128, 512], F32)), Buf()) for i in range(3)])
            ost = Rot([(es.enter_context(nc.sbuf_tensor("ost%d%s" % (i, tag), [128, 512], F32)), Buf()) for i in range(2)])
            return (ps_s, ps_o, pts, rdt, osbs, ost, {"pend": [], "epi": [], "recip_dve": recip_dve})

        for l in range(nlayers):
            x_src = x_in if l == 0 else y
            B_xsrc = B_XIN if l == 0 else B_Y

            es_l = ExitStack()
            kmT = es_l.enter_context(nc.sbuf_tensor("kmT%d" % l, [128, 4, 256], BF16))
            vm = es_l.enter_context(nc.sbuf_tensor("vm%d" % l, [128, 2, 512], BF16))
            B_kmT, B_vm = Buf(), Buf()
            with ExitStack() as es:
                def sb(name, shape, dt=F32):
                    return es.enter_context(nc.sbuf_tensor(name + "_p0_%d" % l, shape, dt))

                def ps(name, shape, dt=F32):
                    return es.enter_context(nc.psum_tensor(name + "_p0_%d" % l, shape, dt))

                wkv = sb("wkv", [128, 8, 1024], BF16)
                B_w = Buf()
                load_weights([(wkv[:, 4 * c:4 * c + 4, :], W["m_wkv"][l][c * 512:(c + 1) * 512, :].rearrange("(c p) n -> p c n", p=128), 128, 4096, 4) for c in range(2)])
                gb = sb("gb", [128, 1024])
                B_g = Buf()
                k.dma("sp", gb[:], W["mem_kv_norm_g"][l].partition_broadcast(128), [], [B_g], sem_ld[0])
                gk = sb("gk", [128, 1])
                load_gcol(gk[:, 0:1], W["m_k_norm_g"][l], B_g, 128)
                mt = sb("mt", [128, 2, 1024])
                B_mt = Buf()
                k.dma("sp", mt[:], mem_in.rearrange("(n p) d -> p n d", p=128), [], [B_mt], sem_ld[1])
                mmT = sb("mmT", [128, 8, 256], BF16)
                B_mmT = Buf()
                pT = ps("pT", [128, 8, 128], BF16)
                B_pT = Buf()
                nt = norm_transpose(es, "p0_%d" % l)
                for b in range(2):
                    nt(mt[:, b, :], B_mt, gb, B_g, mmT[:, :, b * 128:(b + 1) * 128], B_mmT, pT, B_pT)
                stats, apply, rope, qns = headnorm_factory(es, "p0_%d" % l, 256)
                pp = Rot([(ps("pp%d" % i, [128, 512]), Buf()) for i in range(2)])
                pss = Rot([(ps("pss%d" % i, [128, 512]), Buf()) for i in range(2)])
                for h in range(4):
                    p_, B_p = pp.next()
                    for c in range(8):
                        k.op("pe", lambda e, c=c, p_=p_: e.matmul(p_[:, 0:256], lhsT=wkv[:, c, h * 128:(h + 1) * 128], rhs=mmT[:, c, :], start=(c == 0), stop=(c == 7)),
                             [B_w, B_mmT], [B_p])
                    s_, B_s = pss.next()
                    rs, B_rs = stats(128, [(p_[:, 0:256], B_p)], ones_b[:, :], 128.0, C_ZERO, s_[:, 0:256], B_s)
                    apply(128, p_[:, 0:256], B_p, gk[:, 0:1], B_g, rs, B_rs, kmT[:, h, :], B_kmT)
                for b in range(2):
                    p_, B_p = pp.next()
                    for c in range(8):
                        k.op("pe", lambda e, c=c, p_=p_: e.matmul(p_[:], lhsT=mmT[:, c, b * 128:(b + 1) * 128], rhs=wkv[:, c, 512:1024], start=(c == 0), stop=(c == 7)),
                             [B_w, B_mmT], [B_p])
                    k.op("act", lambda e, p_=p_, b=b: e.activation(out=vm[:, b, :], in_=p_[:], func=AF.Copy), [B_p], [B_vm])

            if phases is not None and "p1" not in phases:
                es_l.close()
                continue
            k.barrier()
            with ExitStack() as es:
                def sb(name, shape, dt=F32):
                    return es.enter_context(nc.sbuf_tensor(name + "_p1_%d" % l, shape, dt))

                def ps(name, shape, dt=F32):
                    return es.enter_context(nc.psum_tensor(name + "_p1_%d" % l, shape, dt))

                win = sb("win", [128, 8, IN_COLS], BF16)
                wuq = sb("wuq", [128, 2, 768], BF16)
                wukv = sb("wukv", [128, 1024], BF16)
                B_w = Buf()
                load_weights([(win[:, 2 * c:2 * c + 2, :], W["w_in"][l][c * 256:(c + 1) * 256, :].rearrange("(c p) n -> p c n", p=128), 128, 2 * IN_COLS, 2) for c in range(4)]
                             + [(wuq[:, :, :], W["w_uq"][l].rearrange("(c p) n -> p c n", p=128), 128, 2 * 768, 2)]
                             + [(wukv[:], W["w_ukv"][l], 128, 1024)])
                gb = sb("gb", [128, 1024])
                gc = sb("gc", [128, 8])
                B_g = Buf()
                k.dma("sp", gb[:], W["norm_mix_g"][l].partition_broadcast(128), [], [B_g], sem_ld[0])
                load_gcol(gc[0:64, 0:1], W["a_q_norm_g"][l], B_g, 64)
                load_gcol(gc[64:128, 0:1], W["a_q_norm_g"][l], B_g, 64)
                load_gcol(gc[0:64, 1:2], W["a_k_norm_g"][l], B_g, 64)
                load_gcol(gc[64:128, 1:2], W["a_k_norm_g"][l], B_g, 64)
                load_gcol(gc[:, 2:3], W["b_cq_norm_g"][l][0:128], B_g, 128)
                load_gcol(gc[:, 3:4], W["b_cq_norm_g"][l][128:256], B_g, 128)
                load_gcol(gc[:, 4:5], W["b_ckv_norm_g"][l], B_g, 128)
                load_gcol(gc[0:96, 5:6], W["b_q_norm_g"][l], B_g, 96)
                load_gcol(gc[0:96, 6:7], W["b_k_norm_g"][l], B_g, 96)
                onesv = sb("onesv", [128, 512], BF16)
                k.op("pool", lambda e: e.memset(onesv[:], 1.0), [], [B_g])
                wk96 = sb("wk96", [128, 8, 96], BF16)
                k.op("pool", lambda e: e.memset(wk96[:], 0.0), [], [B_w])
                k.op("pool", lambda e: e.tensor_copy(out=wk96[:, :, 0:64], in_=wukv[:].rearrange("p (h two d) -> p h two d", h=8, two=2)[:, :, 0, :]), [B_w], [B_w])

                xts = Rot([(sb("xt%d" % i, [128, 4, 1024]), Buf()) for i in range(3)])
                tabs = Rot([(sb("tab%d" % i, [128, 4, 512]), Buf()) for i in range(3)])
                hTs = Rot([(sb("hT%d" % i, [128, 8, 512], BF16), Buf()) for i in range(2)])
                cqn = sb("cqn", [128, 2, 512], BF16)
                ckvn = sb("ckvn", [128, 512], BF16)
                krT = sb("krT", [32, 512], BF16)
                B_cqn, B_ckvn, B_krT = Buf(), Buf(), Buf()
                outs = Rot([(sb("qo%d" % i, [128, 512], BF16), Buf()) for i in range(3)])
                vst = Rot([(sb("vst%d" % i, [128, 8, 65], BF16), Buf()) for i in range(2)])
                pT = ps("pT", [128, 8, 128], BF16)
                B_pT = Buf()
                pp = Rot([(ps("pp%d" % i, [128, 512]), Buf()) for i in range(3)])
                pss = Rot([(ps("pss%d" % i, [128, 512]), Buf()) for i in range(2)])
                prq = Rot([(ps("prq%d" % i, [128, 512]), Buf()) for i in range(2)])
                nt = norm_transpose(es, "p1_%d" % l)
                stats, apply, rope, qns = headnorm_factory(es, "p1_%d" % l, 512)
                x_t = x_src.rearrange("(n p) d -> p n d", p=128)

                def proj(col0, ncols, hT, B_hT, p_, B_p):
                    for c in range(8):
                        k.op("pe", lambda e, c=c: e.matmul(p_[0:ncols, :], lhsT=win[:, c, col0:col0 + ncols], rhs=hT[:, c, :], start=(c == 0), stop=(c == 7)),
                             [B_w, B_hT], [B_p])

                def v_evac(p_, B_p, blk, dst, B_dst):
                    vs, B_vs = vst.next()
                    k.op("dve", lambda e: e.scalar_tensor_tensor(out=vs[:, :, 1:65], in0=p_[:].rearrange("p (h d) -> p h d", h=8), scalar=kmask[:, blk:blk + 1],
                                                                  in1=onesv[:].rearrange("p (h d) -> p h d", h=8), op0=ALU.mult, op1=ALU.mult),
                         [B_p, B_const, B_g], [B_vs])
                    k.op("dve", lambda e: e.scalar_tensor_tensor(out=vs[:, :, 0:1], in0=onesv[:, 0:8].rearrange("p (h o) -> p h o", o=1), scalar=kmask[:, blk:blk + 1],
                                                                  in1=onesv[:, 0:8].rearrange("p (h o) -> p h o", o=1), op0=ALU.mult, op1=ALU.mult), [B_const, B_g], [B_vs])
                    k.dma("sp", dst[blk * 128:(blk + 1) * 128, :], vs[:].rearrange("p h d -> p (h d)"), [B_vs], [B_dst], sem_st, nowaw=True)

                def p1_load(t):
                    xt, B_xt = xts.next()
                    k.dma("sp", xt[:], x_t[:, t * 4:(t + 1) * 4, :], B_xsrc[2 * t:2 * t + 2], [B_xt], sem_ld[1])
                    tab, B_tab = tabs.next()
                    for i_, n_ in enumerate(["c_ropeA_c", "c_ropeA_s", "c_ropeB_c", "c_ropeB_s"]):
                        k.dma("sp", tab[:, i_, :], C[n_][:, t * 512:(t + 1) * 512], [], [B_tab], sem_ld[2])
                    return xt, B_xt, tab, B_tab

                nxt = p1_load(0)
                def run_pipe(gens):
                    active = []
                    for g in list(gens) + [None] * 6:
                        for a_ in reversed(list(active)):
                            try:
                                next(a_)
                            except StopIteration:
                                active.remove(a_)
                        if g is not None:
                            try:
                                next(g)
                                active.append(g)
                            except StopIteration:
                                pass

                def g_apair(isk, pr, hT, B_hT, tab, B_tab, sl):
                    p_, B_p = pp.next()
                    proj(isk * 512 + pr * 128, 128, hT, B_hT, p_, B_p)
                    yield
                    sql = stats.sq(128, [(p_[:], B_p)])
                    yield
                    s_, B_s = pss.next()
                    rs, B_rs = stats.fin(128, sql, blk2[:, :], 64.0, C_ZERO if isk else C_L64, s_, B_s)
                    qn, B_qn = qns.next()
                    apply(128, p_[:], B_p, gc[:, isk:isk + 1], B_g, rs, B_rs, qn[:], B_qn)
                    yield
                    r_, B_r = prq.next()
                    tu = rope.a(128, qn, B_qn, ra[:, :], tab[:, 0, :], tab[:, 1, :], B_tab, r_, B_r)
                    yield
                    o_, B_o = outs.next()
                    rope.b(128, *tu, o_[:], B_o)
                    dst, B_dst = (KA, B_KA) if isk else (QA, B_QA)
                    k.dma("sp", dst[pr, :, sl], o_[:], [B_o], [B_dst], sem_st, nowaw=True)

                def g_va(b, blk, hT, B_hT):
                    p_, B_p = pp.next()
                    for c in range(8):
                        k.op("pe", lambda e, c=c, p_=p_: e.matmul(p_[:], lhsT=hT[:, c, b * 128:(b + 1) * 128], rhs=win[:, c, 1024:1536], start=(c == 0), stop=(c == 7)),
                             [B_w, B_hT], [B_p])
                    yield
                    v_evac(p_, B_p, blk, VA, B_VA)

                def g_cq(hT, B_hT):
                    p0, B_p0 = pp.next()
                    proj(1536, 128, hT, B_hT, p0, B_p0)
                    p1, B_p1 = pp.next()
                    proj(1664, 128, hT, B_hT, p1, B_p1)
                    yield
                    sql = stats.sq(128, [(p0[:], B_p0), (p1[:], B_p1)])
                    yield
                    s_, B_s = pss.next()
                    rs, B_rs = stats.fin(128, sql, ones_b[:, :], 256.0, C_ZERO, s_, B_s)
                    apply(128, p0[:], B_p0, gc[:, 2:3], B_g, rs, B_rs, cqn[:, 0, :], B_cqn)
                    apply(128, p1[:], B_p1, gc[:, 3:4], B_g, rs, B_rs, cqn[:, 1, :], B_cqn)

                def g_ckv(hT, B_hT):
                    p_, B_p = pp.next()
                    proj(1792, 128, hT, B_hT, p_, B_p)
                    yield
                    sql = stats.sq(128, [(p_[:], B_p)])
                    yield
                    s_, B_s = pss.next()
                    rs, B_rs = stats.fin(128, sql, ones_b[:, :], 128.0, C_ZERO, s_, B_s)
                    apply(128, p_[:], B_p, gc[:, 4:5], B_g, rs, B_rs, ckvn[:], B_ckvn)

                def g_kr(hT, B_hT):
                    p_, B_p = pp.next()
                    proj(1920, 32, hT, B_hT, p_, B_p)
                    yield
                    k.op("act", lambda e, p_=p_: e.activation(out=krT[:], in_=p_[0:32, :], func=AF.Copy), [B_p], [B_krT])

                def g_bhead(isk, h, tab, B_tab, sl):
                    p_, B_p = pp.next()
                    if isk:
                        k.op("pe", lambda e, p_=p_: e.matmul(p_[0:96, :], lhsT=esel[0:32, 0:96], rhs=krT[:], start=True, stop=False), [B_const, B_krT], [B_p])
                        k.op("pe", lambda e, p_=p_: e.matmul(p_[0:96, :], lhsT=wk96[:, h, :], rhs=ckvn[:], start=False, stop=True), [B_w, B_ckvn], [B_p])
                    else:
                        for j in range(2):
                            k.op("pe", lambda e, j=j, p_=p_: e.matmul(p_[0:96, :], lhsT=wuq[:, j, h * 96:(h + 1) * 96], rhs=cqn[:, j, :], start=(j == 0), stop=(j == 1)),
                                 [B_w, B_cqn], [B_p])
                    yield
                    sql = stats.sq(96, [(p_[0:96, :], B_p)])
                    yield
                    s_, B_s = pss.next()
                    rs, B_rs = stats.fin(96, sql, ones_b[0:96, 0:96], 96.0, C_ZERO if isk else C_L96, s_, B_s)
                    qn, B_qn = qns.next()
                    apply(96, p_[0:96, :], B_p, gc[0:96, 5 + isk:6 + isk], B_g, rs, B_rs, qn[0:96, :], B_qn)
                    yield
                    r_, B_r = prq.next()
                    tu = rope.a(96, qn, B_qn, rb[0:96, 0:96], tab[0:96, 2, :], tab[0:96, 3, :], B_tab, r_, B_r)
                    yield
                    o_, B_o = outs.next()
                    rope.b(96, *tu, o_[0:96, :], B_o)
                    dst, B_dst = (KBd, B_KB) if isk else (QB, B_QB)
                    k.dma("sp", dst[h, :, sl], o_[0:96, :], [B_o], [B_dst], sem_st, nowaw=True)

                wv = wukv[:].rearrange("p (h two d) -> p h two d", h=8, two=2)[:, :, 1, :]

                def g_vb(b, blk):
                    p_, B_p = pp.next()
                    k.op("pe", lambda e, p_=p_: e.matmul(p_[:], lhsT=ckvn[:, b * 128:(b + 1) * 128], rhs=wv, start=True, stop=True), [B_w, B_ckvn], [B_p])
                    yield
                    v_evac(p_, B_p, blk, VB, B_VB)

                def g_nt(b, xt_n, B_xt_n, hT_n, B_hT_n):
                    i = nt.pre(xt_n[:, b, :], B_xt_n, gb, B_g)
                    yield
                    yield
                    nt.post(i, hT_n[:, :, b * 128:(b + 1) * 128], B_hT_n, pT, B_pT)

                nxt2 = p1_load(1) if NT > 1 else None
                hT, B_hT = hTs.next()
                for b in range(4):
                    nt(nxt[0][:, b, :], nxt[1], gb, B_g, hT[:, :, b * 128:(b + 1) * 128], B_hT, pT, B_pT)
                for t in range(NT):
                    xt, B_xt, tab, B_tab = nxt
                    if t + 1 < NT:
                        nxt = nxt2
                        nxt2 = p1_load(t + 2) if t + 2 < NT else None
                        hT_n, B_hT_n = hTs.next()
                    sl = slice(t * 512, (t + 1) * 512)
                    gens = [g_cq(hT, B_hT), g_ckv(hT, B_hT), g_kr(hT, B_hT)]
                    ap = [g_apair(isk, pr, hT, B_hT, tab, B_tab, sl) for isk in range(2) for pr in range(4)]
                    for i_, g_ in enumerate(ap):
                        gens.append(g_)
                        if t + 1 < NT and i_ % 2 == 0:
                            gens.append(g_nt(i_ // 2, nxt[0], nxt[1], hT_n, B_hT_n))
                    gens += [g_va(b, t * 4 + b, hT, B_hT) for b in range(4)]
                    gens += [g_bhead(0, h, tab, B_tab, sl) for h in range(8)]
                    gens += [g_bhead(1, h, tab, B_tab, sl) for h in range(8)]
                    gens += [g_vb(b, t * 4 + b) for b in range(4)]
                    run_pipe(gens)
                    if t + 1 < NT:
                        hT, B_hT = hT_n, B_hT_n

            if phases is not None and "p2" not in phases:
                es_l.close()
                continue
            k.barrier()
            with ExitStack() as es:
                def sb(name, shape, dt=F32):
                    return es.enter_context(nc.sbuf_tensor(name + "_p2_%d" % l, shape, dt))

                vall = sb("vall", [128, NB, 8 * 65], BF16)
                B_v = Buf()
                k.dma("sp", vall[:], VB.rearrange("(n p) f -> p n f", p=128), [B_VB], [B_v], sem_ld[3])
                qs = Rot([(sb("q%d" % i, [96, S], BF16), Buf()) for i in range(2)])
                ks = Rot([(sb("k%d" % i, [96, S], BF16), Buf()) for i in range(2)])
                bufs = attn_bufs(es, "_p2_%d" % l, 3, recip_dve=False)
                def p2_load(h):
                    q_, B_q = qs.next()
                    k_, B_k = ks.next()
                    k.dma("sp", q_[:], QB[h], [B_QB], [B_q], sem_ld[4])
                    k.dma("sp", k_[:], KBd[h], [B_KB], [B_k], sem_ld[5])
                    return q_, B_q, k_, B_k

                nxt = p2_load(0)
                for h in range(8):
                    q_, B_q, k_, B_k = nxt
                    if h + 1 < 8:
                        nxt = p2_load(h + 1)
                    for t in range(NT):
                        attention_tile(bufs, 96, q_[:, t * 512:(t + 1) * 512],
                                       lambda j, k_=k_: k_[:, j * 128:(j + 1) * 128],
                                       lambda j, h=h: vall[:, j, h * 65:(h + 1) * 65],
                                       list(range(NB)), None,
                                       OT[4 + h // 2, (h % 2) * 64:(h % 2) * 64 + 64, t * 512:(t + 1) * 512],
                                       [B_q, B_k, B_v], B_OT)

                attention_flush(bufs)
            k.barrier()
            with ExitStack() as es:
                def sb(name, shape, dt=F32):
                    return es.enter_context(nc.sbuf_tensor(name + "_p3_%d" % l, shape, dt))

                vall = sb("vall", [128, NB, 8 * 65], BF16)
                B_v = Buf()
                k.dma("sp", vall[:], VA.rearrange("(n p) f -> p n f", p=128), [B_VA], [B_v], sem_ld[3])
                band = sb("band", [128, 20, 512], BF16)
                B_band = Buf()
                load_weights([(band[:, n_:n_ + 5, :], C["c_band"][n_:n_ + 5].rearrange("c p n -> p c n"), 128, 2560, 5) for n_ in range(0, 20, 5)])
                qz = [sb("qz%d" % i, [128, S], BF16) for i in range(2)]
                B_qz = [Buf(), Buf()]
                k.op("pool", lambda e: e.memset(qz[0][64:128, :], 0.0), [], [B_qz[0]])
                k.op("pool", lambda e: e.memset(qz[1][0:64, :], 0.0), [], [B_qz[1]])
                ks = Rot([(sb("k%d" % i, [128, S], BF16), Buf()) for i in range(2)])
                bufs = attn_bufs(es, "_p3_%d" % l, 3)

                def p3_loadk(pr):
                    k_, B_k = ks.next()
                    k.dma("sp", k_[:], KA[pr], [B_KA], [B_k], sem_ld[5])
                    return k_, B_k

                nxt = p3_loadk(0)
                for pr in range(4):
                    k_, B_k = nxt
                    k.dma("sp", qz[0][0:64, :], QA[pr, 0:64, :], [B_QA], [B_qz[0]], sem_ld[4])
                    k.dma("sp", qz[1][64:128, :], QA[pr, 64:128, :], [B_QA], [B_qz[1]], sem_ld[4])
                    if pr + 1 < 4:
                        nxt = p3_loadk(pr + 1)
                    for hh in range(2):
                        h = pr * 2 + hh
                        rows = slice(hh * 64, hh * 64 + 64)
                        for t in range(NT):
                            kbl = [j for j in range(4 * t - 8, 4 * t + 12) if 0 <= j < NB]
                            attention_tile(bufs, 128, qz[hh][:, t * 512:(t + 1) * 512],
                                           lambda j, k_=k_: k_[:, j * 128:(j + 1) * 128],
                                           lambda j, h=h: vall[:, j, h * 65:(h + 1) * 65],
                                           kbl, lambda grp, t=t: band[:, grp[0] - 4 * t + 8:grp[0] - 4 * t + 8 + len(grp), :],
                                           OT[pr, rows, t * 512:(t + 1) * 512],
                                           [B_qz[hh], B_k, B_v, B_band], B_OT)

                attention_flush(bufs)
            k.barrier()
            with ExitStack() as es:
                def sb(name, shape, dt=F32):
                    return es.enter_context(nc.sbuf_tensor(name + "_p4_%d" % l, shape, dt))

                def ps(name, shape, dt=F32):
                    return es.enter_context(nc.psum_tensor(name + "_p4_%d" % l, shape, dt))

                wout = sb("wout", [128, 8, 1024], BF16)
                mwq = sb("mwq", [128, 8, 512], BF16)
                mwo = sb("mwo", [128, 4, 1024], BF16)
                B_w = Buf()
                load_weights([(wout[:, 4 * c:4 * c + 4, :], W["w_out"][l][c * 512:(c + 1) * 512, :].rearrange("(c p) n -> p c n", p=128), 128, 4096, 4) for c in range(2)]
                             + [(mwq[:, :, :], W["m_wq"][l].rearrange("(c p) n -> p c n", p=128), 128, 4096, 8)]
                             + [(mwo[:, :, :], W["m_wo"][l].rearrange("(c p) n -> p c n", p=128), 128, 4096, 4)])
                gb = sb("gb", [128, 1024])
                gc = sb("gc", [128, 12])
                B_g = Buf()
                k.dma("sp", gb[:], W["norm_mem_g"][l].partition_broadcast(128), [], [B_g], sem_ld[0])
                for j in range(4):
                    load_gcol(gc[:, j:j + 1], W["a_out_norm_g"][l][j * 128:(j + 1) * 128], B_g, 128)
                    load_gcol(gc[:, 4 + j:5 + j], W["b_out_norm_g"][l][j * 128:(j + 1) * 128], B_g, 128)
                load_gcol(gc[:, 8:9], W["m_q_norm_g"][l], B_g, 128)

                xts = Rot([(sb("xt%d" % i, [128, 4, 1024]), Buf()) for i in range(2)])
                ots = Rot([(sb("ot%d" % i, [128, 8, 512]), Buf()) for i in range(2)])
                mixT = sb("mixT", [128, 8, 512], BF16)
                B_mix = Buf()
                hT = sb("hT", [128, 8, 512], BF16)
                B_hT = Buf()
                omT = sb("omT", [128, 4, 512], BF16)
                B_om = Buf()
                pts = Rot([(sb("pt%d" % i, [128, 512], BF16), Buf()) for i in range(4)])
                rds = Rot([(sb("rdm%d" % i, [128, 512]), Buf()) for i in range(2)])
                pT = ps("pT", [128, 8, 128], BF16)
                B_pT = Buf()
                pp = Rot([(ps("pp%d" % i, [128, 512]), Buf()) for i in range(2)])
                pq = Rot([(ps("pq%d" % i, [128, 512]), Buf()) for i in range(2)])
                hsq = Rot([(sb("hsq%d" % i, [128, 512], BF16), Buf()) for i in range(4)])
                pss = Rot([(ps("pss%d" % i, [128, 512]), Buf()) for i in range(1)])
                pso = Rot([(ps("pso%d" % i, [128, 512]), Buf()) for i in range(2)])
                nt = norm_transpose(es, "p4_%d" % l)
                stats, apply, rope, qns = headnorm_factory(es, "p4_%d" % l, 512)
                x_t = x_src.rearrange("(n p) d -> p n d", p=128)
                y_t = y.rearrange("(n p) d -> p n d", p=128)
                def p4_load(t):
                    sl = slice(t * 512, (t + 1) * 512)
                    xt, B_xt = xts.next()
                    k.dma("sp", xt[:], x_t[:, t * 4:(t + 1) * 4, :], B_xsrc[2 * t:2 * t + 2], [B_xt], sem_ld[1])
                    ot, B_ot = ots.next()
                    k.dma("sp", ot[:], OT[:, :, sl].rearrange("n p f -> p n f"), [B_OT], [B_ot], sem_ld[2])
                    return xt, B_xt, ot, B_ot

                def h1_onorm(T):
                    ot, B_ot = T["ot"], T["B_ot"]
                    for grp in range(2):
                        s_, B_s = pss.next()
                        rs, B_rs = stats(128, [(ot[:, grp * 4 + j, :], B_ot) for j in range(4)], ones_b[:, :], 512.0, C_ZERO, s_, B_s)
                        for j in range(4):
                            apply(128, ot[:, grp * 4 + j, :], B_ot, gc[:, grp * 4 + j:grp * 4 + j + 1], B_g, rs, B_rs, mixT[:, grp * 4 + j, :], B_mix)

                def h1_wout(T, groups):
                    xt, B_xt = T["xt"], T["B_xt"]
                    for (b, hf) in groups:
                        p_, B_p = pp.next()
                        for j in range(8):
                            k.op("pe", lambda e, j=j, p_=p_: e.matmul(p_[:], lhsT=mixT[:, j, b * 128:(b + 1) * 128], rhs=wout[:, j, hf * 512:(hf + 1) * 512], start=(j == 0), stop=(j == 7)),
                                 [B_w, B_mix], [B_p])
                        k.op("dve", lambda e, p_=p_: e.tensor_tensor(out=xt[:, b, hf * 512:(hf + 1) * 512], in0=xt[:, b, hf * 512:(hf + 1) * 512], in1=p_[:], op=ALU.add),
                             [B_p, B_xt], [B_xt])

                def h1_nt(T):
                    for b in range(4):
                        nt(T["xt"][:, b, :], T["B_xt"], gb, B_g, hT[:, :, b * 128:(b + 1) * 128], B_hT, pT, B_pT)

                def hA1(h):
                    p_, B_p = pq.next()
                    for c in range(8):
                        k.op("pe", lambda e, c=c, p_=p_: e.matmul(p_[:], lhsT=mwq[:, c, h * 128:(h + 1) * 128], rhs=hT[:, c, :], start=(c == 0), stop=(c == 7)),
                             [B_w, B_hT], [B_p])
                    sq, B_sq = hsq.next()
                    k.op("act", lambda e, sq=sq, p_=p_: e.activation(out=sq[:], in_=p_[:], func=AF.Square), [B_p], [B_sq])
                    return p_, B_p, [(sq, B_sq)]

                def hA2(h, p_, B_p, sql):
                    s_, B_s = pss.next()
                    rs, B_rs = stats.fin(128, sql, ones_b[:, :], 128.0, C_L128, s_, B_s)
                    qn, B_qn = qns.next()
                    apply(128, p_[:], B_p, gc[:, 8:9], B_g, rs, B_rs, qn[:], B_qn)
                    return qn, B_qn

                def hB1(h, qn, B_qn):
                    ptl = []
                    for kb_ in range(2):
                        pS, B_pS = pp.next()
                        k.op("pe", lambda e, pS=pS, kb_=kb_: e.matmul(pS[:], lhsT=kmT[:, h, kb_ * 128:(kb_ + 1) * 128], rhs=qn[:], start=True, stop=True), [B_kmT, B_qn], [B_pS])
                        pt, B_pt = pts.next()
                        k.op("act", lambda e, pS=pS, pt=pt: e.activation(out=pt[:], in_=pS[:], func=AF.Exp), [B_pS], [B_pt])
                        ptl.append((pt, B_pt))
                    return ptl

                def hB2(h, ptl):
                    po, B_po = pso.next()
                    pd, B_pd = pso.next()
                    for kb_, (pt, B_pt) in enumerate(ptl):
                        k.op("pe", lambda e, pt=pt, kb_=kb_: e.matmul(po[:], lhsT=vm[:, kb_, h * 128:(h + 1) * 128], rhs=pt[:], start=(kb_ == 0), stop=(kb_ == 1)), [B_vm, B_pt], [B_po])
                    for kb_, (pt, B_pt) in enumerate(ptl):
                        k.op("pe", lambda e, pt=pt, kb_=kb_: e.matmul(pd[:], lhsT=ones_b[:, :], rhs=pt[:], start=(kb_ == 0), stop=(kb_ == 1)), [B_const, B_pt], [B_pd])
                    rd, B_rd = rds.next()
                    k.op("act", lambda e, rd=rd: e.activation(out=rd[:], in_=pd[:], func=AF.Ln, bias=C_TINY, scale=1.0), [B_pd, B_const], [B_rd])
                    k.op("act", lambda e, rd=rd: e.activation(out=rd[:], in_=rd[:], func=AF.Exp, bias=C_ZERO, scale=-1.0), [B_rd, B_const], [B_rd])
                    k.op("dve", lambda e, rd=rd: e.tensor_tensor(out=omT[:, h, :], in0=po[:], in1=rd[:], op=ALU.mult), [B_po, B_rd], [B_om])

                def h2_mwo(T):
                    xt, B_xt = T["xt"], T["B_xt"]
                    for b in range(4):
                        for hf in range(2):
                            p_, B_p = pp.next()
                            for j in range(4):
                                k.op("pe", lambda e, j=j, p_=p_: e.matmul(p_[:], lhsT=omT[:, j, b * 128:(b + 1) * 128], rhs=mwo[:, j, hf * 512:(hf + 1) * 512], start=(j == 0), stop=(j == 3)),
                                     [B_w, B_om], [B_p])
                            k.op("dve", lambda e, p_=p_: e.tensor_tensor(out=xt[:, b, hf * 512:(hf + 1) * 512], in0=xt[:, b, hf * 512:(hf + 1) * 512], in1=p_[:], op=ALU.add),
                                 [B_p, B_xt], [B_xt])

                def p4_tile(t):
                    xt, B_xt, ot, B_ot = p4_load(t)
                    return dict(xt=xt, B_xt=B_xt, ot=ot, B_ot=B_ot)

                WG = [(b, hf) for b in range(4) for hf in range(2)]
                T = p4_tile(0)
                h1_onorm(T)
                h1_wout(T, WG)
                h1_nt(T)
                for t in range(NT):
                    Tn = p4_tile(t + 1) if t + 1 < NT else None
                    a0 = hA1(0)
                    a1 = hA1(1)
                    q0 = hA2(0, *a0)
                    a2 = hA1(2)
                    e0 = hB1(0, *q0)
                    q1 = hA2(1, *a1)
                    a3 = hA1(3)
                    hB2(0, e0)
                    e1 = hB1(1, *q1)
                    q2 = hA2(2, *a2)
                    if Tn:
                        h1_onorm(Tn)
                    hB2(1, e1)
                    e2 = hB1(2, *q2)
                    q3 = hA2(3, *a3)
                    if Tn:
                        h1_wout(Tn, WG[0:3])
                    hB2(2, e2)
                    e3 = hB1(3, *q3)
                    if Tn:
                        h1_wout(Tn, WG[3:6])
                    hB2(3, e3)
                    if Tn:
                        h1_wout(Tn, WG[6:8])
                        h1_nt(Tn)
                    h2_mwo(T)
                    k.dma("sp", y_t[:, t * 4:(t + 1) * 4, :], T["xt"][:], [T["B_xt"]], B_Y[2 * t:2 * t + 2], sem_st)
                    T = Tn
            es_l.close()

            k.barrier()
            if phases is not None and "p5" not in phases:
                continue
            with ExitStack() as es:
                def sb(name, shape, dt=F32):
                    return es.enter_context(nc.sbuf_tensor(name + "_p5_%d" % l, shape, dt))

                def ps(name, shape, dt=F32):
                    return es.enter_context(nc.psum_tensor(name + "_p5_%d" % l, shape, dt))

                w1 = sb("w1", [128, 8, DFF], BF16)
                w2 = sb("w2", [128, 32, 1024], BF16)
                B_w = Buf()
                load_weights([(w1[:, c, :], W["w_ff1"][l][c * 128:(c + 1) * 128, :], 128, DFF) for c in range(8)]
                             + [(w2[:, 4 * c:4 * c + 4, :], W["w_ff2"][l][c * 512:(c + 1) * 512, :].rearrange("(c p) n -> p c n", p=128), 128, 4096, 4) for c in range(8)])
                gb = sb("gb", [128, 1024])
                B_g = Buf()
                k.dma("sp", gb[:], W["norm_ffn_g"][l].partition_broadcast(128), [], [B_g], sem_ld[0])
                xts = Rot([(sb("xt%d" % i, [128, 2, 1024]), Buf()) for i in range(2)])
                hT = sb("hT", [128, 8, 256], BF16)
                B_hT = Buf()
                aT = sb("aT", [128, 32, 256], BF16)
                B_aT = Buf()
                rls = Rot([(sb("rl%d" % i, [128, 512]), Buf()) for i in range(2)])
                pT = ps("pT", [128, 8, 128], BF16)
                B_pT = Buf()
                pp = Rot([(ps("pp%d" % i, [128, 512]), Buf()) for i in range(3)])
                po = Rot([(ps("po%d" % i, [128, 512]), Buf()) for i in range(3)])
                nt = norm_transpose(es, "p5_%d" % l)
                y_t = y.rearrange("(n p) d -> p n d", p=128)
                def p5_load(t):
                    xt, B_xt = xts.next()
                    k.dma("sp", xt[:], y_t[:, t * 2:(t + 1) * 2, :], [B_Y[t]], [B_xt], sem_ld[1])
                    return xt, B_xt

                nxt = p5_load(0)
                NT5 = S // 256
                for b in range(2):
                    nt(nxt[0][:, b, :], nxt[1], gb, B_g, hT[:, :, b * 128:(b + 1) * 128], B_hT, pT, B_pT)
                for t in range(NT5):
                    xt, B_xt = nxt
                    if t + 1 < NT5:
                        nxt = p5_load(t + 1)
                    for f2 in range(16):
                        p_, B_p = pp.next()
                        for u in range(2):
                            fc = f2 * 2 + u
                            for c in range(8):
                                k.op("pe", lambda e, c=c, fc=fc, u=u, p_=p_: e.matmul(p_[:, u * 256:(u + 1) * 256], lhsT=w1[:, c, fc * 128:(fc + 1) * 128], rhs=hT[:, c, :], start=(c == 0), stop=(c == 7)),
                                     [B_w, B_hT], [B_p])
                        rl, B_rl = rls.next()
                        k.op("act", lambda e, p_=p_, rl=rl: e.activation(out=rl[:], in_=p_[:], func=AF.Relu), [B_p], [B_rl])
                        k.op("pool", lambda e, rl=rl, f2=f2: e.tensor_tensor(out=aT[:, f2 * 2:f2 * 2 + 2, :], in0=rl[:].rearrange("p (u n) -> p u n", u=2),
                                                                          in1=rl[:].rearrange("p (u n) -> p u n", u=2), op=ALU.mult), [B_rl], [B_aT])
                    pre_i = []
                    if t + 1 < NT5:
                        pre_i = [nt.pre(nxt[0][:, b, :], nxt[1], gb, B_g) for b in range(2)]
                    gi = 0
                    for b in range(2):
                        for hf in range(2):
                            p_, B_p = po.next()
                            for fc in range(32):
                                k.op("pe", lambda e, fc=fc, p_=p_: e.matmul(p_[:], lhsT=aT[:, fc, b * 128:(b + 1) * 128], rhs=w2[:, fc, hf * 512:(hf + 1) * 512], start=(fc == 0), stop=(fc == 31)),
                                     [B_w, B_aT], [B_p])
                            k.op("dve", lambda e, p_=p_: e.tensor_tensor(out=xt[:, b, hf * 512:(hf + 1) * 512], in0=xt[:, b, hf * 512:(hf + 1) * 512], in1=p_[:], op=ALU.add),
                                 [B_p, B_xt], [B_xt])
                            if gi == 1:
                                for b2, i_ in enumerate(pre_i):
                                    nt.post(i_, hT[:, :, b2 * 128:(b2 + 1) * 128], B_hT, pT, B_pT)
                            gi += 1
                    k.dma("sp", y_t[:, t * 2:(t + 1) * 2, :], xt[:], [B_xt], [B_Y[t]], sem_st)
            k.barrier()

        k.barrier()
    return nc


_NC_CACHE = {}


def run_cores(xs, mems, nvalid, weights, S, debug=False, **bkw):
    key = (S, debug, tuple(sorted(bkw.items())))
    if key not in _NC_CACHE:
        _NC_CACHE[key] = build(S, debug=debug, **bkw)
    nc = _NC_CACHE[key]
    consts = host_consts(S)
    in_maps = []
    for x, m, nv in zip(xs, mems, nvalid):
        km = (np.arange(S) < nv).astype(np.float32).reshape(S // 128, 128).T.copy()
        d = {"x": np.ascontiguousarray(x, dtype=np.float32), "kmask": km, "mem": np.ascontiguousarray(m, dtype=np.float32)}
        d.update(weights)
        d.update(consts)
        in_maps.append(d)
    res = run_bass_kernel_spmd(nc, in_maps, core_ids=list(range(len(xs))))
    return res.results


def kernel(**inputs):
    weights = {n: np.ascontiguousarray(inputs[n], dtype=np.float32) for n, _ in WNAMES}
    xp = np.asarray(inputs["x_prompt"], dtype=np.float32)
    xs_ = np.asarray(inputs["x_sample"], dtype=np.float32)
    mp = np.asarray(inputs["mem_prompt"], dtype=np.float32)
    ms = np.asarray(inputs["mem_sample"], dtype=np.float32)
    S = SMAX
    xs, mems, nvalid = [], [], []
    for b in range(xp.shape[0]):
        pad = np.zeros((S, D), np.float32)
        pad[:xp.shape[1]] = xp[b]
        xs.append(pad)
        mems.append(mp[b])
        nvalid.append(xp.shape[1])
    for b in range(xs_.shape[0]):
        xs.append(xs_[b])
        mems.append(ms[b])
        nvalid.append(xs_.shape[1])
    res = run_cores(xs, mems, nvalid, weights, S)
    y_prompt = np.stack([res[b]["y"][:xp.shape[1]] for b in range(xp.shape[0])]).astype(np.float32)
    y_sample = np.stack([res[xp.shape[0] + b]["y"] for b in range(xs_.shape[0])]).astype(np.float32)
    return (y_prompt, y_sample)
```
